# Optimizing a Trainium2 kernel written in Bass

```python
import jax, jax.numpy as jnp
from jax import lax
import numpy as np

D_MODEL = 2048
BATCH = 32
SEQ = 256
DEPTH = 2
DEC_BATCH = 8
DEC_SEQ = 1024
PAST_LEN = 256

GRID_W = 64
RET_HEADS = 8
RET_DK = 128
RET_DV = 256
CHUNK = 128
GM_GROUPS = 4
GM_WIDTH = 1024
GM_GROUP_DIM = GM_WIDTH // GM_GROUPS
NA_HEADS = 8
NA_HEAD_DIM = 128
NA_WIN_ROWS = 8
NA_WIN_COLS = 16
FFN_DIM = 5632
N_SUB = 3
ROPE_BASE = 10000.0
NORM_EPS = 1e-6
Q_BLOCK = 128
NEG_INF = -1e30
RET_QK_W = RET_HEADS * RET_DK
RET_W = RET_HEADS * RET_DV
NA_W = NA_HEADS * NA_HEAD_DIM
IN_SIZES = (RET_QK_W, RET_QK_W, RET_W, RET_W, GM_WIDTH, GM_WIDTH, NA_W, NA_W, NA_W, D_MODEL, D_MODEL, D_MODEL)
IN_WIDTH = 2 * RET_QK_W + 2 * RET_W + 2 * GM_WIDTH + 3 * NA_W + 3 * D_MODEL

kernel_name = "hybrid_dit_retention_gmlp_natten_step"


def rms_norm(x, w):
    xf = x.astype(jnp.float32)
    y = xf * lax.rsqrt(jnp.mean(xf * xf, axis=-1, keepdims=True) + NORM_EPS)
    return (y * w).astype(x.dtype)


def layer_norm(x, w):
    xf = x.astype(jnp.float32)
    mu = jnp.mean(xf, axis=-1, keepdims=True)
    var = jnp.mean(jnp.square(xf - mu), axis=-1, keepdims=True)
    return ((xf - mu) * lax.rsqrt(var + NORM_EPS) * w).astype(x.dtype)


def mod_part(mod, i, j):
    s = (3 * i + j) * D_MODEL
    return mod[..., s:s + D_MODEL]


def modulated_input(x, mod, i, g_pre):
    return rms_norm(x, g_pre) * (1.0 + mod_part(mod, i, 1)) + mod_part(mod, i, 0)


def gated_residual(x, mod, i, out, g_post, res_w):
    return x + res_w * mod_part(mod, i, 2) * rms_norm(out, g_post)


def swiglu(h, w_in, w_out):
    a, b = jnp.split(h @ w_in, 2, axis=-1)
    return (jax.nn.silu(a) * b) @ w_out


def ffn_sublayer(x, mod, i, g_pre, g_post, w_in, w_out):
    h = modulated_input(x, mod, i, g_pre)
    return gated_residual(x, mod, i, swiglu(h, w_in, w_out), g_post, 0.5)


def split_projection(z):
    out, start = [], 0
    for size in IN_SIZES:
        out.append(z[..., start:start + size])
        start += size
    return out


def to_heads(x, n_heads):
    return x.reshape(x.shape[0], x.shape[1], n_heads, -1)


def axial_rotary(x):
    L, d = x.shape[1], x.shape[-1]
    t = jnp.arange(L)
    pos_r = (t // GRID_W).astype(jnp.float32)
    pos_c = (t % GRID_W).astype(jnp.float32)
    half = d // 2
    inv = ROPE_BASE ** (-jnp.arange(0, half, 2, dtype=jnp.float32) / half)

    def rotate(xa, pos):
        ang = pos[:, None] * inv[None, :]
        cos = jnp.cos(ang)[None, :, None, :]
        sin = jnp.sin(ang)[None, :, None, :]
        x1, x2 = xa[..., :half // 2], xa[..., half // 2:]
        return jnp.concatenate([x1 * cos - x2 * sin, x1 * sin + x2 * cos], axis=-1)

    return jnp.concatenate([rotate(x[..., :half], pos_r), rotate(x[..., half:], pos_c)], axis=-1).astype(x.dtype)


def retention_dir(q, k, v, log_g, s0):
    Bn, L, H, dk = q.shape
    dv = v.shape[-1]
    n = L // CHUNK
    qc = q.reshape(Bn, n, CHUNK, H, dk)
    kc = k.reshape(Bn, n, CHUNK, H, dk)
    vc = v.reshape(Bn, n, CHUNK, H, dv)
    pos = jnp.arange(CHUNK, dtype=jnp.float32)
    diff = pos[:, None] - pos[None, :]
    dmask = jnp.where(diff >= 0, jnp.exp(log_g[:, None, None] * jnp.maximum(diff, 0.0)), 0.0)
    scores = jnp.einsum('bnihd,bnjhd->bnhij', qc, kc) * dmask[None, None]
    inner = jnp.einsum('bnhij,bnjhe->bnihe', scores, vc)
    zeta = jnp.exp(log_g[:, None] * (CHUNK - 1.0 - pos)[None, :])
    kv = jnp.einsum('bnjhd,hj,bnjhe->bnhde', kc, zeta, vc)
    g_chunk = jnp.exp(log_g * CHUNK)[None, :, None, None]

    def step(s, kv_i):
        return g_chunk * s + kv_i, s

    s_final, s_prev = lax.scan(step, s0.astype(kv.dtype), jnp.moveaxis(kv, 1, 0))
    s_prev = jnp.moveaxis(s_prev, 0, 1)
    xi = jnp.exp(log_g[:, None] * (pos + 1.0)[None, :])
    cross = jnp.einsum('bnihd,bnhde->bnihe', qc, s_prev) * xi.T[None, None, :, :, None]
    return (inner + cross).reshape(Bn, L, H, dv), s_final


def retention_branch(q, k, v, g, log_g, s0_f, s0_b):
    Bn, L = q.shape[0], q.shape[1]
    k = k * (RET_DK ** -0.5)
    o_f, s_f = retention_dir(q, k, v, log_g[0], s0_f)
    o_b, s_b = retention_dir(jnp.flip(q, 1), jnp.flip(k, 1), jnp.flip(v, 1), log_g[1], s0_b)
    o = (o_f + jnp.flip(o_b, 1)).astype(jnp.float32)
    o = o * lax.rsqrt(jnp.mean(o * o, axis=-1, keepdims=True) + NORM_EPS)
    o = o.astype(q.dtype) * jax.nn.silu(g.reshape(Bn, L, RET_HEADS, RET_DV))
    return o.reshape(Bn, L, RET_W), s_f, s_b


def spatial_gating(u, v, gm_norm, gm_ws, gm_bs):
    Bn, L, _ = u.shape
    n = L // CHUNK
    vn = layer_norm(v, gm_norm).reshape(Bn, n, CHUNK, GM_GROUPS, GM_GROUP_DIM)
    mixed = jnp.einsum('gij,bnjgc->bnigc', gm_ws, vn) + gm_bs.T[None, None, :, :, None]
    return u * mixed.reshape(Bn, L, GM_WIDTH)


def dense_attention(q, k, v):
    Bn, Lq, H, d = q.shape
    scale = d ** -0.5
    qb = jnp.moveaxis(q.reshape(Bn, Lq // Q_BLOCK, Q_BLOCK, H, d), 1, 0)

    def one_block(qblk):
        s = jnp.einsum('bqhd,bkhd->bhqk', qblk, k).astype(jnp.float32) * scale
        p = jax.nn.softmax(s, axis=-1).astype(v.dtype)
        return jnp.einsum('bhqk,bkhd->bqhd', p, v)

    o = lax.map(one_block, qb)
    return jnp.moveaxis(o, 0, 1).reshape(Bn, Lq, H * d)


def neighbourhood_attention(q, k, v, k_ctx, v_ctx, rpb):
    Bn, L, H, d = q.shape
    rows = L // GRID_W
    wr = min(NA_WIN_ROWS, rows)
    scale = d ** -0.5
    r = jnp.arange(rows)
    row_start = jnp.clip(r - wr // 2, 0, rows - wr)
    row_idx = row_start[:, None] + jnp.arange(wr)[None, :]
    qg = q.reshape(Bn, rows, GRID_W, H, d)
    kb = jnp.take(k.reshape(Bn, rows, GRID_W, H, d), row_idx, axis=1)
    vb = jnp.take(v.reshape(Bn, rows, GRID_W, H, d), row_idx, axis=1)
    s_loc = jnp.einsum('brqhd,brjkhd->brhqjk', qg, kb).astype(jnp.float32) * scale
    cidx = jnp.arange(GRID_W)
    col_start = jnp.clip(cidx - NA_WIN_COLS // 2, 0, GRID_W - NA_WIN_COLS)
    col_valid = (cidx[None, :] >= col_start[:, None]) & (cidx[None, :] < col_start[:, None] + NA_WIN_COLS)
    col_i = jnp.clip(cidx[None, :] - cidx[:, None] + NA_WIN_COLS - 1, 0, 2 * NA_WIN_COLS - 2)
    row_off = row_idx - r[:, None] + NA_WIN_ROWS - 1
    bias = rpb[:, row_off][..., col_i]
    bias = jnp.transpose(bias, (1, 0, 3, 2, 4)).astype(jnp.float32)
    s_loc = jnp.where(col_valid[:, None, :], s_loc + bias[None], NEG_INF)
    s_ctx = jnp.einsum('brqhd,bkhd->brhqk', qg, k_ctx).astype(jnp.float32) * scale
    n_loc = wr * GRID_W
    s_all = jnp.concatenate([s_loc.reshape(Bn, rows, H, GRID_W, n_loc), s_ctx], axis=-1)
    p = jax.nn.softmax(s_all, axis=-1).astype(v.dtype)
    p_loc = p[..., :n_loc].reshape(Bn, rows, H, GRID_W, wr, GRID_W)
    p_ctx = p[..., n_loc:]
    o = jnp.einsum('brhqjk,brjkhd->brqhd', p_loc, vb) + jnp.einsum('brhqk,bkhd->brqhd', p_ctx, v_ctx)
    return o.reshape(Bn, L, H * d)


def merge_branches(ret, gm, na, g_a, g_b, g_c, w_br, w_bg, w_bn, w_out):
    m = jax.nn.sigmoid(g_a) * (ret @ w_br) + jax.nn.sigmoid(g_b) * (gm @ w_bg) + jax.nn.sigmoid(g_c) * (na @ w_bn)
    return m @ w_out


def mix_context(h, w_in, ret_logit, gm_norm, gm_ws, gm_bs, w_br, w_bg, w_bn, w_out):
    Bn = h.shape[0]
    qa, ka, va, ga, ub, vb, qc, kc, vc, g_a, g_b, g_c = split_projection(h @ w_in)
    log_g = jax.nn.log_sigmoid(ret_logit.astype(jnp.float32))
    s0 = jnp.zeros((Bn, RET_HEADS, RET_DK, RET_DV), jnp.float32)
    ret, s_f, s_b = retention_branch(to_heads(qa, RET_HEADS), to_heads(ka, RET_HEADS), to_heads(va, RET_HEADS), ga, log_g, s0, s0)
    gm = spatial_gating(ub, vb, gm_norm, gm_ws, gm_bs)
    k_c = to_heads(kc, NA_HEADS)
    v_c = to_heads(vc, NA_HEADS)
    na = dense_attention(to_heads(qc, NA_HEADS), k_c, v_c)
    y = merge_branches(ret, gm, na, g_a, g_b, g_c, w_br, w_bg, w_bn, w_out)
    return y, k_c, v_c, jnp.stack([s_f, s_b], axis=1)


def mix_latent(h, ctx_k, ctx_v, ret_state, w_in, ret_logit, gm_norm, gm_ws, gm_bs, rpb, w_br, w_bg, w_bn, w_out):
    qa, ka, va, ga, ub, vb, qc, kc, vc, g_a, g_b, g_c = split_projection(h @ w_in)
    log_g = jax.nn.log_sigmoid(ret_logit.astype(jnp.float32))
    ret, _, _ = retention_branch(axial_rotary(to_heads(qa, RET_HEADS)), axial_rotary(to_heads(ka, RET_HEADS)),
                                 to_heads(va, RET_HEADS), ga, log_g, ret_state[:, 0], ret_state[:, 1])
    gm = spatial_gating(ub, vb, gm_norm, gm_ws, gm_bs)
    na = neighbourhood_attention(to_heads(qc, NA_HEADS), to_heads(kc, NA_HEADS), to_heads(vc, NA_HEADS), ctx_k, ctx_v, rpb)
    return merge_branches(ret, gm, na, g_a, g_b, g_c, w_br, w_bg, w_bn, w_out)


def setup_inputs(seed: int = 0) -> dict:
    key = jax.random.key(seed)
    ks = jax.random.split(key, 24)
    D = D_MODEL

    def nrm(k, shape, s):
        return jax.random.normal(k, shape, jnp.float32) * s

    gamma0 = 1.0 - 2.0 ** (-5.0 - jnp.arange(RET_HEADS, dtype=jnp.float32))
    logit0 = jnp.log(gamma0) - jnp.log1p(-gamma0)
    return {
        "x_prompt": nrm(ks[0], (BATCH, SEQ, D), 1.0),
        "x_sample": nrm(ks[1], (DEC_BATCH, DEC_SEQ, D), 1.0),
        "cache_k": nrm(ks[2], (DEC_BATCH, DEPTH, PAST_LEN, NA_HEADS, NA_HEAD_DIM), 1.0),
        "cache_v": nrm(ks[3], (DEC_BATCH, DEPTH, PAST_LEN, NA_HEADS, NA_HEAD_DIM), 1.0),
        "state_ret": nrm(ks[4], (DEC_BATCH, DEPTH, 2, RET_HEADS, RET_DK, RET_DV), 1.0),
        "c": nrm(ks[5], (DEC_BATCH, D), 1.0),
        "c_ctx": nrm(ks[6], (D,), 1.0),
        "w_ada": nrm(ks[7], (DEPTH, D, 3 * N_SUB * D), 0.5 * D ** -0.5),
        "b_ada": nrm(ks[8], (DEPTH, 3 * N_SUB * D), 0.02),
        "norm_pre": 1.0 + nrm(ks[9], (DEPTH, N_SUB, D), 0.05),
        "norm_post": 1.0 + nrm(ks[10], (DEPTH, N_SUB, D), 0.05),
        "ffn_w_in": nrm(ks[11], (DEPTH, 2, D, 2 * FFN_DIM), D ** -0.5),
        "ffn_w_out": nrm(ks[12], (DEPTH, 2, FFN_DIM, D), FFN_DIM ** -0.5),
        "w_in": nrm(ks[13], (DEPTH, D, IN_WIDTH), D ** -0.5),
        "ret_decay_logit": logit0[None, None, :] + nrm(ks[14], (DEPTH, 2, RET_HEADS), 0.1),
        "gm_norm": 1.0 + nrm(ks[15], (DEPTH, GM_WIDTH), 0.05),
        "gm_ws": nrm(ks[16], (DEPTH, GM_GROUPS, CHUNK, CHUNK), CHUNK ** -0.5),
        "gm_bs": nrm(ks[17], (DEPTH, GM_GROUPS, CHUNK), 0.02),
        "na_rpb": nrm(ks[18], (DEPTH, NA_HEADS, 2 * NA_WIN_ROWS - 1, 2 * NA_WIN_COLS - 1), 0.1),
        "w_branch_ret": nrm(ks[19], (DEPTH, RET_W, D), RET_W ** -0.5),
        "w_branch_gm": nrm(ks[20], (DEPTH, GM_WIDTH, D), GM_WIDTH ** -0.5),
        "w_branch_na": nrm(ks[21], (DEPTH, NA_W, D), NA_W ** -0.5),
        "w_out": nrm(ks[22], (DEPTH, D, D), D ** -0.5),
    }


def reference(x_prompt, x_sample, cache_k, cache_v, state_ret, c, c_ctx,
              w_ada, b_ada, norm_pre, norm_post, ffn_w_in, ffn_w_out, w_in,
              ret_decay_logit, gm_norm, gm_ws, gm_bs, na_rpb,
              w_branch_ret, w_branch_gm, w_branch_na, w_out):
    x = x_prompt
    ks_out, vs_out, ss_out = [], [], []
    for l in range(DEPTH):
        mod = (jax.nn.silu(c_ctx) @ w_ada[l] + b_ada[l])[None, None, :]
        x = ffn_sublayer(x, mod, 0, norm_pre[l, 0], norm_post[l, 0], ffn_w_in[l, 0], ffn_w_out[l, 0])
        h = modulated_input(x, mod, 1, norm_pre[l, 1])
        y, k_c, v_c, s_c = mix_context(h, w_in[l], ret_decay_logit[l], gm_norm[l], gm_ws[l], gm_bs[l],
                                       w_branch_ret[l], w_branch_gm[l], w_branch_na[l], w_out[l])
        x = gated_residual(x, mod, 1, y, norm_post[l, 1], 1.0)
        x = ffn_sublayer(x, mod, 2, norm_pre[l, 2], norm_post[l, 2], ffn_w_in[l, 1], ffn_w_out[l, 1])
        ks_out.append(k_c)
        vs_out.append(v_c)
        ss_out.append(s_c)
    y_prompt = x
    new_cache_k = jnp.stack(ks_out, axis=1)
    new_cache_v = jnp.stack(vs_out, axis=1)
    new_state_ret = jnp.stack(ss_out, axis=1)

    x = x_sample
    for l in range(DEPTH):
        mod = (jax.nn.silu(c) @ w_ada[l] + b_ada[l])[:, None, :]
        x = ffn_sublayer(x, mod, 0, norm_pre[l, 0], norm_post[l, 0], ffn_w_in[l, 0], ffn_w_out[l, 0])
        h = modulated_input(x, mod, 1, norm_pre[l, 1])
        y = mix_latent(h, cache_k[:, l], cache_v[:, l], state_ret[:, l], w_in[l], ret_decay_logit[l],
                       gm_norm[l], gm_ws[l], gm_bs[l], na_rpb[l],
                       w_branch_ret[l], w_branch_gm[l], w_branch_na[l], w_out[l])
        x = gated_residual(x, mod, 1, y, norm_post[l, 1], 1.0)
        x = ffn_sublayer(x, mod, 2, norm_pre[l, 2], norm_post[l, 2], ffn_w_in[l, 1], ffn_w_out[l, 1])
    y_sample = x
    return (y_prompt, y_sample, new_cache_k, new_cache_v, new_state_ret)
```

```python
import os
import numpy as np
import ml_dtypes
from contextlib import ExitStack
import concourse.bass as bass
import concourse.mybir as mybir
from concourse.bass_utils import run_bass_kernel_spmd

F32 = mybir.dt.float32
BF16 = mybir.dt.bfloat16
AF = mybir.ActivationFunctionType
ALU = mybir.AluOpType
AX = mybir.AxisListType

D = 2048
T = 1024
NCH = 16
FF = 5632
NF = 44
INW = 17408
EPS = 1e-6
RQ, RK, RV, RG, GU, GV, NQ, NK, NV, GA, GB, GC = 0, 1024, 2048, 4096, 6144, 7168, 8192, 9216, 10240, 11264, 13312, 15360
SC_RET = 128.0 ** -0.5
NEGB = -30000.0

C_IDF, C_PERM, C_CR, C_SR, C_DP, C_DN, C_MGE, C_MLE, C_NEGM, C_POS = 0, 128, 256, 320, 384, 512, 640, 768, 896, 960
NCF = 964


class Trk:
    __slots__ = ("w", "r", "sid", "dc")

    def __init__(self):
        self.w = {}
        self.r = {}
        self.sid = None
        self.dc = 0


class Prog:
    def __init__(self, nc, es):
        self.nc = nc
        self.es = es
        self.sems = []
        self.eng = {}
        for name, e in (("pe", nc.tensor), ("act", nc.scalar), ("dve", nc.vector), ("pool", nc.gpsimd), ("sp", nc.sync)):
            sid = self.newsem(name)
            self.eng[name] = dict(e=e, sid=sid, count=0, seen={})
        self.dma_tot = {}
        self.uid = 0
        self.nins = 0

    def newsem(self, name):
        h = self.es.enter_context(self.nc.semaphore("s%s%d" % (name, len(self.sems))))
        self.sems.append(h)
        return len(self.sems) - 1

    def _need(self, E, r, w, skip=None):
        need = {}
        own = E["sid"]
        for b in r:
            for sid, v in b.w.items():
                if need.get(sid, 0) < v:
                    need[sid] = v
        for b in w:
            for sid, v in b.w.items():
                if sid != own and need.get(sid, 0) < v:
                    need[sid] = v
            for sid, v in b.r.items():
                if sid != own and need.get(sid, 0) < v:
                    need[sid] = v
        seen = E["seen"]
        for sid, v in need.items():
            if sid == skip:
                continue
            if seen.get(sid, 0) < v:
                E["e"].wait_ge(self.sems[sid], v)
                seen[sid] = v

    def op(self, en, fn, r=(), w=(), signal=True):
        E = self.eng[en]
        self._need(E, r, w)
        ins = fn()
        self.nins += 1
        if signal:
            ins.then_inc(self.sems[E["sid"]], 1)
            E["count"] += 1
            tok = E["count"]
        else:
            tok = E["count"] + 1
        sid = E["sid"]
        for b in r:
            if b.r.get(sid, 0) < tok:
                b.r[sid] = tok
        for b in w:
            b.w[sid] = tok
        return ins

    def dma(self, en, out, in_, semb, r=(), w=(), nonc=False):
        E = self.eng[en]
        if semb.sid is None or semb.dc >= 8000:
            semb.sid = self.newsem("d")
            semb.dc = 0
        self._need(E, r, w, skip=(semb.sid if w else None))
        if nonc:
            ins = E["e"].dma_start(out=out, in_=in_, allow_slow_non_contiguous=True)
        else:
            ins = E["e"].dma_start(out=out, in_=in_)
        self.nins += 1
        ins.then_inc(self.sems[semb.sid], 16)
        semb.dc += 16
        self.dma_tot[semb.sid] = semb.dc
        for b in r:
            if b.r.get(semb.sid, 0) < semb.dc:
                b.r[semb.sid] = semb.dc
        for b in w:
            b.w[semb.sid] = semb.dc
        return ins

    def barrier(self, names=("pe", "act", "dve", "sp")):
        for a in names:
            A = self.eng[a]
            for b in names:
                if a == b:
                    continue
                B = self.eng[b]
                if A["seen"].get(B["sid"], 0) < B["count"]:
                    A["e"].wait_ge(self.sems[B["sid"]], B["count"])
                    A["seen"][B["sid"]] = B["count"]

    def finish(self):
        sp = self.eng["sp"]
        for sid, tot in self.dma_tot.items():
            if sp["seen"].get(sid, 0) < tot:
                sp["e"].wait_ge(self.sems[sid], tot)
                sp["seen"][sid] = tot
        for n in ("pe", "act", "dve", "pool"):
            B = self.eng[n]
            if B["count"] > 0 and sp["seen"].get(B["sid"], 0) < B["count"]:
                sp["e"].wait_ge(self.sems[B["sid"]], B["count"])


class Stop(Exception):
    pass


KSTOP = os.environ.get("KSTOP", "")


def stop_at(name):
    if KSTOP == name:
        raise Stop()


class Region:
    def __init__(self, P, base, size):
        self.P = P
        self.base = base
        self.size = size
        self.off = 0

    def reset(self, pool=False):
        self.P.barrier(("pe", "act", "dve", "sp", "pool") if pool else ("pe", "act", "dve", "sp"))
        self.off = 0

    def alloc(self, shape, dt):
        n = 1
        for s in shape[1:]:
            n *= s
        nb = n * (4 if dt == F32 else 2)
        nb = (nb + 31) // 32 * 32
        assert self.off + nb <= self.size, ("region overflow", self.off, nb, self.size)
        self.P.uid += 1
        t = self.P.nc.alloc_sbuf_tensor_at("r%d" % self.P.uid, list(shape), dt, offset=self.base + self.off)
        self.off += nb
        return t


def build_program(dbg=None):
    nc = bass.Bass("TRN2", target_bir_lowering=False)

    def din(name, shape, dt=F32):
        return nc.dram_tensor(name, list(shape), dt, kind="ExternalInput").ap()

    def dout(name, shape):
        return nc.dram_tensor(name, list(shape), F32, kind="ExternalOutput").ap()

    xin = [din("xp", [T, D]), din("xs", [T, D])]
    ck = din("ck", [2, 256, 1024])
    cv = din("cv", [2, 256, 1024])
    sr = din("sr", [2, 2, 8, 128, 256])
    c2 = din("c2", [2, D])
    w_ada = din("w_ada", [2, D, 9 * D])
    b_ada = din("b_ada", [2, 9 * D])
    norm_pre = din("norm_pre", [2, 3, D])
    norm_post = din("norm_post", [2, 3, D])
    ffn_w_in = din("ffn_w_in", [2, 2, D, 2 * FF])
    ffn_w_out = din("ffn_w_out", [2, 2, FF, D])
    w_in = din("w_in", [2, D, INW])
    rdl = din("ret_decay_logit", [2, 16])
    gm_norm = din("gm_norm", [2, 1024])
    gm_ws = din("gm_ws", [2, 4, 128, 128])
    gm_bs = din("gm_bs", [2, 4, 128])
    na_rpb = din("na_rpb", [2, 8, 15, 31])
    w_br = din("w_branch_ret", [2, D, D])
    w_bg = din("w_branch_gm", [2, 1024, D])
    w_bn = din("w_branch_na", [2, 1024, D])
    w_o = din("w_out", [2, D, D])
    cstf = din("cstf", [128, NCF])
    cstb = din("cstb", [128, 256], BF16)
    ehot = din("ehot", [32, 4096], BF16)
    yout = [dout("yp", [T, D]), dout("ys", [T, D])]
    nk = dout("nk", [4, 2, 256, 1024])
    nv = dout("nv", [4, 2, 256, 1024])
    ns = dout("ns", [4, 2, 2, 8, 128, 256])
    dbg_out = dout("dbg", [128, NCH * T]) if dbg else None

    es = ExitStack()
    with es:
        arena = es.enter_context(nc.sbuf_tensor("arena", [128, 212000 // 4], F32))
        P = Prog(nc, es)
        SB0 = 16512
        XT_OFF, AR_OFF, TMP_OFF, WR_OFF, CST_OFF = SB0, SB0 + 65536, SB0 + 163840, SB0 + 186368, SB0 + 202752

        def at(name, shape, dt, off):
            P.uid += 1
            return nc.alloc_sbuf_tensor_at("%s%d" % (name, P.uid), list(shape), dt, offset=off)

        XTt = at("xt", [128, NCH, T], F32, XT_OFF)
        XT = [Trk() for _ in range(NCH)]
        HTt = at("ht", [128, NCH, T], BF16, AR_OFF)
        BRt = at("br", [128, NCH, T], BF16, AR_OFF + 32768)
        Mt = at("m", [128, NCH, T], BF16, AR_OFF + 65536)
        Yt = at("y", [128, NCH, T], F32, AR_OFF)
        HT = [Trk() for _ in range(NCH)]
        BR = [Trk() for _ in range(NCH)]
        MM = [Trk() for _ in range(NCH)]
        YY = [Trk() for _ in range(NCH)]
        HHt = at("hh", [128, NCH, 512], BF16, AR_OFF)
        Gt = at("g", [128, NF, 512], BF16, AR_OFF + 16384)
        YFt = at("yf", [128, NCH, 512], F32, AR_OFF + 61440)
        HH = [Trk() for _ in range(NCH)]
        GG = [Trk() for _ in range(NF)]
        YF = [Trk() for _ in range(NCH)]
        RTR = Region(P, AR_OFF + 65536, 32768)
        BRU = Region(P, AR_OFF + 32768 + 16384, 16384)
        TMP = Region(P, TMP_OFF, 22528)
        WSL = [at("wsl", [128, 4096], BF16, WR_OFF + 8192 * i) for i in range(2)]
        WTR = [(Trk(), Trk()) for _ in range(2)]
        co = [CST_OFF]

        def calloc(shape, dt):
            n = 1
            for s in shape[1:]:
                n *= s
            nb = (n * (4 if dt == F32 else 2) + 31) // 32 * 32
            t = at("c", shape, dt, co[0])
            co[0] += nb
            assert co[0] <= SB0 + 212000, co[0]
            return t

        CF = calloc([128, NCF], F32)
        CB = calloc([128, 256], BF16)
        MOD = calloc([128, 2, 9, NCH, 2], F32)
        NPRE = calloc([128, 6, NCH], F32)
        NPOST = calloc([128, 6, NCH], F32)
        NLG = calloc([128, 2, 16], F32)
        RTAB = calloc([128, 6, 16], F32)
        ACO = calloc([128, NCH], F32)
        COEF = calloc([128, NCH], F32)
        SMALL = calloc([128, 64], F32)
        CST = Trk()
        DT = {}

        def dtrk(name):
            if name not in DT:
                DT[name] = Trk()
            return DT[name]
        TAB = Trk()
        ACT_ = Trk()
        IDF = CF[:, C_IDF:C_IDF + 128]
        PERM = CF[:, C_PERM:C_PERM + 128]
        IDB = CB[:, 0:128]
        ONESB = CB[:, 128:256]

        PSt = [nc.alloc_psum_tensor("ps%d" % i, [128, 512], F32) for i in range(8)]
        PSK = [Trk() for _ in range(8)]
        psi = [0]
        psn = [7]
        pso = [0]

        def nps():
            i = psi[0] % psn[0]
            psi[0] = (i + 1) % psn[0]
            return PSt[i], PSK[i]

        def npo():
            i = 5 + pso[0]
            pso[0] ^= 1
            return PSt[i], PSK[i]

        PSS, PSSK = PSt[7], PSK[7]

        V = lambda fn, r=(), w=(): P.op("dve", fn, r, w)
        A = lambda fn, r=(), w=(): P.op("act", fn, r, w)

        def MMs(ps_ap, psk, lhsT, rhs, start, stop, r, sig=None):
            return P.op("pe", lambda: nc.tensor.matmul(ps_ap, lhsT=lhsT, rhs=rhs, start=start, stop=stop), r, (psk,), signal=(stop if sig is None else sig))

        def TR(ps_ap, psk, in_, ident, r):
            return P.op("pe", lambda: nc.tensor.transpose(ps_ap, in_, ident), r, (psk,))

        wq = []
        wstate = dict(issued=0, taken=0, ready=[])

        def wpush(parts):
            tot = sum(p[2] * p[4] for p in parts)
            assert tot <= 4096, tot
            wq.append(parts)
            wkick()

        def _wissue():
            i = wstate["issued"]
            parts = wq[i]
            slot = i % 2
            views = []
            lo, hi = WTR[slot]
            split = (len(parts) == 2 and all(p[2] * p[4] <= 2048 for p in parts))
            off = 0
            for pi, (dr, r0, nkk, c0, ncols) in enumerate(parts):
                if split:
                    off = pi * 2048
                dst = WSL[slot][:, off:off + nkk * ncols].rearrange("p (k c) -> p k c", c=ncols)
                src = dr[r0:r0 + nkk * 128, c0:c0 + ncols].rearrange("(k p) c -> p k c", p=128)
                if split:
                    tk_ = lo if pi == 0 else hi
                    P.dma("pool", dst, src, tk_, r=(), w=(tk_,))
                else:
                    P.dma("pool", dst, src, lo, r=(), w=(lo, hi))
                views.append(dst)
                off += nkk * ncols
            wstate["ready"].append((views, (lo, hi)))
            wstate["issued"] = i + 1

        def wkick():
            while wstate["issued"] < min(len(wq), wstate["taken"] + 2):
                _wissue()

        def wtake():
            n = wstate["taken"]
            while wstate["issued"] < min(len(wq), n + 2):
                _wissue()
            wstate["taken"] = n + 1
            return wstate["ready"][n]

        P.dma("sp", CF[:], cstf, CST, w=(CST,))
        P.dma("sp", CB[:], cstb, CST, w=(CST,))
        P.dma("sp", NPRE[:], norm_pre.rearrange("l s (c p) -> p (l s) c", p=128), CST, w=(CST,), nonc=True)
        P.dma("sp", NPOST[:], norm_post.rearrange("l s (c p) -> p (l s) c", p=128), CST, w=(CST,), nonc=True)
        TMP.reset()
        BADA = TMP.alloc([128, 2, 144], F32)
        SCF = TMP.alloc([128, 2, NCH], F32)
        SCB = TMP.alloc([128, NCH, 2], BF16)
        LGR = TMP.alloc([128, 32], F32)
        P.dma("sp", BADA[:], b_ada.rearrange("l (j p) -> p l j", p=128), CST, w=(CST,), nonc=True)
        for r_ in range(2):
            P.dma("sp", SCF[:, r_, :], c2[r_].rearrange("(c p) -> p c", p=128), CST, w=(CST,), nonc=True)
        P.dma("sp", LGR[:], rdl.rearrange("l x -> (l x)").rearrange("(o x) -> o x", o=1).broadcast_to([128, 32]), CST, w=(CST,), nonc=True)
        STP = Trk()
        A(lambda: nc.scalar.activation(out=SCB[:].rearrange("p c r -> p r c"), in_=SCF[:], func=AF.Silu), r=(CST,), w=(STP,))
        A(lambda: nc.scalar.activation(out=LGR[:], in_=LGR[:], func=AF.Exp, scale=-1.0), r=(CST,), w=(STP,))
        A(lambda: nc.scalar.activation(out=NLG[:].rearrange("p l x -> p (l x)"), in_=LGR[:], func=AF.Ln, bias=1.0), r=(STP,), w=(TAB,))

        for l in range(2):
            for jb in range(72):
                wpush([(w_ada[l], 0, 16, jb * 256, 256)])
        for l in range(2):
            pst, psk = nps()
            for jb in range(72):
                (wv,), wt = wtake()
                for jj in range(2):
                    j = jb * 2 + jj
                    for k in range(16):
                        MMs(pst[:, 2 * j:2 * j + 2], psk, wv[:, k, jj * 128:(jj + 1) * 128], SCB[:, k, :], k == 0, k == 15, wt + (STP,))
            V(lambda: nc.vector.tensor_tensor(out=MOD[:, l].rearrange("p a c k -> p (a c) k"),
                                              in0=pst[:, 0:288].rearrange("p (j k) -> p j k", k=2),
                                              in1=BADA[:, l, :].unsqueeze(2).broadcast_to([128, 144, 2]), op=ALU.add),
              r=(psk, CST), w=(TAB,))
        P.barrier()

        def rsq(ap, trk):
            A(lambda: nc.scalar.activation(out=ap, in_=ap, func=AF.Sqrt), r=(trk,), w=(trk,))
            V(lambda: nc.vector.reciprocal(out=ap, in_=ap), r=(trk,), w=(trk,))

        def stats_rstd(src_fn, trks, rstd_ap, rtrk, sq_pool):
            for c in range(NCH):
                sq, sqk = sq_pool[c % 2]
                A(lambda: nc.scalar.activation(out=sq, in_=src_fn(c), func=AF.Square), r=(trks[c],), w=(sqk,))
                MMs(PSS[:], PSSK, ONESB, sq, c == 0, c == NCH - 1, (sqk, CST), sig=True)
            V(lambda: nc.vector.tensor_scalar(out=rstd_ap, in0=PSS[:], scalar1=1.0 / D, scalar2=EPS, op0=ALU.mult, op1=ALU.add), r=(PSSK,), w=(rtrk,))
            rsq(rstd_ap, rtrk)

        def mod_coefs(l, sub, kind, resw):
            V(lambda: nc.vector.scalar_tensor_tensor(out=ACO[:], in0=MOD[:, l, 3 * sub + 1, :, kind], scalar=1.0, in1=NPRE[:, l * 3 + sub, :], op0=ALU.add, op1=ALU.mult),
              r=(TAB, CST), w=(ACT_,))
            V(lambda: nc.vector.scalar_tensor_tensor(out=COEF[:], in0=MOD[:, l, 3 * sub + 2, :, kind], scalar=float(resw), in1=NPOST[:, l * 3 + sub, :], op0=ALU.mult, op1=ALU.mult),
              r=(TAB, CST), w=(ACT_,))

        def norm_mod(l, sub, kind, half, dst_t, dst_trk, dcol0, bufs=None):
            tk = slice(half * 512, half * 512 + 512)
            if bufs is None:
                TMP.reset()
                RS = TMP.alloc([128, 512], F32); RSK = Trk()
                sqp = [(TMP.alloc([128, 512], BF16), Trk()) for _ in range(2)]
                tp = [(TMP.alloc([128, 512], F32), Trk()) for _ in range(2)]
            else:
                RS, RSK, sqp, tp = bufs
            stats_rstd(lambda c: XTt[:, c, tk], XT, RS[:], RSK, [(s[:], k) for s, k in sqp])
            for c in range(NCH):
                tt, ttk = tp[c % 2]
                V(lambda: nc.vector.tensor_tensor(out=tt[:], in0=XTt[:, c, tk], in1=RS[:], op=ALU.mult), r=(XT[c], RSK), w=(ttk,))
                A(lambda: nc.scalar.activation(out=dst_t[:, c, dcol0:dcol0 + 512], in_=tt[:], func=AF.Identity,
                                               bias=MOD[:, l, 3 * sub, c, kind:kind + 1], scale=ACO[:, c:c + 1]),
                  r=(ttk, ACT_, TAB), w=(dst_trk[c],))

        def post_resid(half, ysrc_fn, ytrks, bufs=None):
            tk = slice(half * 512, half * 512 + 512)
            if bufs is None:
                RS = TMP.alloc([128, 512], F32); RSK = Trk()
                tp = [(TMP.alloc([128, 512], F32), Trk()) for _ in range(2)]
            else:
                RS, RSK, tp = bufs
            V(lambda: nc.vector.tensor_scalar(out=RS[:], in0=PSS[:], scalar1=1.0 / D, scalar2=EPS, op0=ALU.mult, op1=ALU.add), r=(PSSK,), w=(RSK,))
            rsq(RS[:], RSK)
            for c in range(NCH):
                tt, ttk = tp[c % 2]
                V(lambda: nc.vector.tensor_tensor(out=tt[:], in0=ysrc_fn(c), in1=RS[:], op=ALU.mult), r=(ytrks[c], RSK), w=(ttk,))
                V(lambda: nc.vector.scalar_tensor_tensor(out=XTt[:, c, tk], in0=tt[:], scalar=COEF[:, c:c + 1], in1=XTt[:, c, tk], op0=ALU.mult, op1=ALU.add),
                  r=(ttk, ACT_, XT[c]), w=(XT[c],))

        def ffn(l, which, kind):
            sub = 0 if which == 0 else 2
            w1 = ffn_w_in[l, which]
            w2 = ffn_w_out[l, which]
            P.barrier()
            mod_coefs(l, sub, kind, 0.5)
            for half in range(2):
                for j in range(NF):
                    wpush([(w1, 0, 16, j * 128, 128), (w1, 0, 16, FF + j * 128, 128)])
                for c in range(NCH):
                    wpush([(w2, 0, 11, c * 128, 128), (w2, 11 * 128, 11, c * 128, 128)])
                    wpush([(w2, 22 * 128, 11, c * 128, 128), (w2, 33 * 128, 11, c * 128, 128)])
            wkick()
            TMP.reset()
            fRS = TMP.alloc([128, 512], F32); fRSK = Trk()
            fsq = [(TMP.alloc([128, 512], BF16), Trk()) for _ in range(2)]
            ftp = [(TMP.alloc([128, 512], F32), Trk()) for _ in range(2)]
            fRS2 = TMP.alloc([128, 512], F32); fRS2K = Trk()
            ftp2 = [(TMP.alloc([128, 512], F32), Trk()) for _ in range(2)]
            sap = [(TMP.alloc([128, 512], F32), Trk()) for _ in range(2)]
            sqp = [(TMP.alloc([128, 512], BF16), Trk()) for _ in range(2)]
            for half in range(2):
                norm_mod(l, sub, kind, half, HHt, HH, 0, bufs=(fRS, fRSK, fsq, ftp))
                stop_at("nm")
                for j in range(NF):
                    (wa, wb), wt = wtake()
                    pa, pak = nps()
                    pb, pbk = nps()
                    for k in range(16):
                        MMs(pa[:], pak, wa[:, k, :], HHt[:, k, :], k == 0, k == 15, (wt[0], HH[k]))
                    for k in range(16):
                        MMs(pb[:], pbk, wb[:, k, :], HHt[:, k, :], k == 0, k == 15, (wt[1], HH[k]))
                    sa, sak = sap[j % 2]
                    A(lambda: nc.scalar.activation(out=sa[:], in_=pa[:], func=AF.Silu), r=(pak,), w=(sak,))
                    V(lambda: nc.vector.tensor_tensor(out=Gt[:, j, :], in0=sa[:], in1=pb[:], op=ALU.mult), r=(sak, pbk), w=(GG[j],))
                stop_at("g1")
                for c in range(NCH):
                    py, pyk = nps()
                    for hh in range(2):
                        (wv0, wv1), wt = wtake()
                        for k in range(22):
                            j = hh * 22 + k
                            wv = wv0 if k < 11 else wv1
                            MMs(py[:], pyk, wv[:, k % 11, :], Gt[:, j, :], j == 0, j == NF - 1, (wt[0 if k < 11 else 1], GG[j]), sig=(True if k in (10, 21) else None))
                    if os.environ.get("KVAR") == "noevac":
                        continue
                    V(lambda: nc.vector.tensor_copy(out=YFt[:, c, :], in_=py[:]), r=(pyk,), w=(YF[c],))
                    if os.environ.get("KVAR") == "nostat":
                        continue
                    sq, sqk = sqp[c % 2]
                    A(lambda: nc.scalar.activation(out=sq[:], in_=YFt[:, c, :], func=AF.Square), r=(YF[c],), w=(sqk,))
                    if os.environ.get("KVAR") == "nomm":
                        continue
                    MMs(PSS[:], PSSK, ONESB, sq[:], c == 0, c == NCH - 1, (sqk, CST), sig=True)
                stop_at("g2")
                post_resid(half, lambda c: YFt[:, c, :], YF, bufs=(fRS2, fRS2K, ftp2))
                stop_at("h0")

        def proj_fm(view, wt, ncols128, evac):
            for cb in range(ncols128):
                for half in range(2):
                    ps, psk = nps()
                    for k in range(16):
                        MMs(ps[:], psk, view[:, k, cb * 128:(cb + 1) * 128], HTt[:, k, half * 512:(half + 1) * 512], k == 0, k == 15, wt + (HT[k],))
                    evac(cb, half, ps, psk)

        def proj_tm(view, wt, ncols, evac):
            for tt in range(8):
                ps, psk = nps()
                for k in range(16):
                    MMs(ps[:, 0:ncols], psk, HTt[:, k, tt * 128:(tt + 1) * 128], view[:, k, :], k == 0, k == 15, wt + (HT[k],))
                evac(tt, ps, psk)

        def attn_core(Mq, q_ap, qtrk, ksegs, vtiles, out_ap, outk, first_out, W_, SALL, SALLK, PB, PBK, PTT, PTK, sm, smk):
            off = 0
            for (k_ap, n, ktrk, bias, btrk) in ksegs:
                ps, psk = nps()
                MMs(ps[0:Mq, 0:n], psk, q_ap, k_ap, True, True, (qtrk, ktrk))
                if bias is not None:
                    V(lambda: nc.vector.tensor_tensor(out=SALL[0:Mq, off:off + n].rearrange("p (j k) -> p j k", k=64),
                                                      in0=ps[0:Mq, 0:n].rearrange("p (j k) -> p j k", k=64), in1=bias, op=ALU.add),
                      r=(psk, btrk), w=(SALLK,))
                else:
                    A(lambda: nc.scalar.copy(out=SALL[0:Mq, off:off + n], in_=ps[0:Mq, 0:n]), r=(psk,), w=(SALLK,))
                off += n
            assert off == W_
            V(lambda: nc.vector.reduce_max(out=sm[0:Mq, 0:1], in_=SALL[0:Mq, 0:W_], axis=AX.X), r=(SALLK,), w=(smk,))
            V(lambda: nc.vector.tensor_scalar(out=sm[0:Mq, 1:2], in0=sm[0:Mq, 0:1], scalar1=-1.0, scalar2=None, op0=ALU.mult), r=(smk,), w=(smk,))
            A(lambda: nc.scalar.activation(out=SALL[0:Mq, 0:W_], in_=SALL[0:Mq, 0:W_], func=AF.Exp, bias=sm[0:Mq, 1:2], scale=1.0, accum_out=sm[0:Mq, 2:3]),
              r=(SALLK, smk), w=(SALLK, smk))
            V(lambda: nc.vector.reciprocal(out=sm[0:Mq, 3:4], in_=sm[0:Mq, 2:3]), r=(smk,), w=(smk,))
            V(lambda: nc.vector.tensor_scalar(out=PB[0:Mq, 0:W_], in0=SALL[0:Mq, 0:W_], scalar1=sm[0:Mq, 3:4], scalar2=None, op0=ALU.mult), r=(SALLK, smk), w=(PBK,))
            nt = W_ // 128
            ps, psk = nps()
            psb = ps[:, :].bitcast(BF16)
            for t in range(nt):
                TR(psb[:, t * Mq:(t + 1) * Mq], psk, PB[0:Mq, t * 128:(t + 1) * 128], IDB[0:Mq, 0:Mq], (PBK, CST))
            A(lambda: nc.scalar.copy(out=PTT[:, 0:nt * Mq], in_=psb[:, 0:nt * Mq]), r=(psk,), w=(PTK,))
            for t in range(nt):
                v_ap, vtrk = vtiles[t]
                MMs(out_ap, outk, v_ap, PTT[:, t * Mq:(t + 1) * Mq], t == 0, t == nt - 1, (vtrk, PTK))

        def mixer(l, seg):
            kind = seg
            wl = w_in[l]
            P.barrier()
            mod_coefs(l, 1, kind, 1.0)
            TBK = Trk()
            posc = [C_POS + 0, C_POS + 1, C_POS + 2, C_POS + 3]
            for (ti, lo, pc, mul) in ((0, 0, posc[0], SC_RET), (0, 8, posc[1], SC_RET), (1, 0, posc[2], 1.0), (1, 8, posc[3], 1.0)):
                V(lambda: nc.vector.tensor_scalar(out=RTAB[:, ti, lo:lo + 8], in0=NLG[:, l, lo:lo + 8], scalar1=CF[:, pc:pc + 1], scalar2=-1.0, op0=ALU.mult, op1=ALU.mult),
                  r=(TAB, CST), w=(TBK,))
            V(lambda: nc.vector.tensor_scalar(out=RTAB[:, 2, :], in0=NLG[:, l, :], scalar1=-128.0, scalar2=None, op0=ALU.mult), r=(TAB,), w=(TBK,))
            A(lambda: nc.scalar.activation(out=RTAB[:, 0:3, :], in_=RTAB[:, 0:3, :], func=AF.Exp), r=(TBK,), w=(TBK,))
            V(lambda: nc.vector.tensor_scalar(out=RTAB[:, 0, :], in0=RTAB[:, 0, :], scalar1=SC_RET, scalar2=None, op0=ALU.mult), r=(TBK,), w=(TBK,))
            for half in range(2):
                norm_mod(l, 1, kind, half, HTt, HT, half * 512)
            P.barrier()
            stop_at("mnm")
            retention(l, seg, TBK)
            P.barrier()
            stop_at("ret")
            merge(l, w_br[l], 16, GA, True)
            P.barrier()
            stop_at("mg1")
            gmlp(l)
            P.barrier()
            stop_at("gm")
            merge(l, w_bg[l], 8, GB, False)
            P.barrier()
            stop_at("mg2")
            attention(l, seg)
            P.barrier()
            stop_at("att")
            merge(l, w_bn[l], 8, GC, False)
            P.barrier()
            stop_at("mg3")
            for cc in range(8):
                wpush([(w_o[l], 0, 16, cc * 256, 256)])
            TMP.reset()
            sqp = [(TMP.alloc([128, 512], BF16), Trk()) for _ in range(2)]
            PS2, PS2K = PSt[6], PSK[6]
            statb = [(PSS, PSSK), (PS2, PS2K)]
            n = 0
            psn[0] = 6
            psi[0] = 0
            for cc in range(8):
                (wv,), wt = wtake()
                for c2_ in range(2):
                    c = cc * 2 + c2_
                    for half in range(2):
                        py, pyk = nps()
                        for k in range(16):
                            MMs(py[:], pyk, wv[:, k, c2_ * 128:(c2_ + 1) * 128], Mt[:, k, half * 512:(half + 1) * 512], k == 0, k == 15, wt + (MM[k],))
                        V(lambda: nc.vector.tensor_copy(out=Yt[:, c, half * 512:(half + 1) * 512], in_=py[:]), r=(pyk,), w=(YY[c],))
                        sq, sqk = sqp[n % 2]
                        n += 1
                        A(lambda: nc.scalar.activation(out=sq[:], in_=Yt[:, c, half * 512:(half + 1) * 512], func=AF.Square), r=(YY[c],), w=(sqk,))
                        sb, sbk = statb[half]
                        MMs(sb[:], sbk, ONESB, sq[:], c == 0, c == NCH - 1, (sqk, CST), sig=True)
            psi[0] = 0
            psn[0] = 7
            for half in range(2):
                tk = slice(half * 512, half * 512 + 512)
                sb, sbk = statb[half]
                RS = TMP.alloc([128, 512], F32); RSK = Trk()
                tp = [(TMP.alloc([128, 512], F32), Trk()) for _ in range(2)]
                V(lambda: nc.vector.tensor_scalar(out=RS[:], in0=sb[:], scalar1=1.0 / D, scalar2=EPS, op0=ALU.mult, op1=ALU.add), r=(sbk,), w=(RSK,))
                rsq(RS[:], RSK)
                for c in range(NCH):
                    tt, ttk = tp[c % 2]
                    V(lambda: nc.vector.tensor_tensor(out=tt[:], in0=Yt[:, c, tk], in1=RS[:], op=ALU.mult), r=(YY[c], RSK), w=(ttk,))
                    V(lambda: nc.vector.scalar_tensor_tensor(out=XTt[:, c, tk], in0=tt[:], scalar=COEF[:, c:c + 1], in1=XTt[:, c, tk], op0=ALU.mult, op1=ALU.add),
                      r=(ttk, ACT_, XT[c]), w=(XT[c],))

        def merge(l, Wb, K, gcol, first):
            wl = w_in[l]
            for c in range(NCH):
                wpush([(wl, 0, 16, gcol + c * 128, 128), (Wb, 0, K, c * 128, 128)])
            TMP.reset()
            sgp = [(TMP.alloc([128, 512], F32), Trk()) for _ in range(2)]
            tp = [(TMP.alloc([128, 512], F32), Trk()) for _ in range(2)]
            n = 0
            for c in range(NCH):
                (wg, wb), wt = wtake()
                for half in range(2):
                    tk = slice(half * 512, half * 512 + 512)
                    pg, pgk = nps()
                    pp, ppk = nps()
                    for k in range(16):
                        MMs(pg[:], pgk, wg[:, k, :], HTt[:, k, tk], k == 0, k == 15, (wt[0], HT[k]))
                    for k in range(K):
                        MMs(pp[:], ppk, wb[:, k, :], BRt[:, k, tk], k == 0, k == K - 1, (wt[1], BR[k]))
                    sg, sgk = sgp[n % 2]
                    A(lambda: nc.scalar.activation(out=sg[:], in_=pg[:], func=AF.Sigmoid), r=(pgk,), w=(sgk,))
                    if first:
                        V(lambda: nc.vector.tensor_tensor(out=Mt[:, c, tk], in0=sg[:], in1=pp[:], op=ALU.mult), r=(sgk, ppk), w=(MM[c],))
                    else:
                        tt, ttk = tp[n % 2]
                        V(lambda: nc.vector.tensor_tensor(out=tt[:], in0=sg[:], in1=pp[:], op=ALU.mult), r=(sgk, ppk), w=(ttk,))
                        V(lambda: nc.vector.tensor_tensor(out=Mt[:, c, tk], in0=tt[:], in1=Mt[:, c, tk], op=ALU.add), r=(ttk, MM[c]), w=(MM[c],))
                    n += 1

        def retention(l, seg, TBK):
            wl = w_in[l]
            nseq, nchs = (4, 2) if seg == 0 else (1, 8)
            for hd in range(8):
                wpush([(wl, 0, 16, RQ + hd * 128, 128), (wl, 0, 16, RK + hd * 128, 128)])
                wpush([(wl, 0, 16, RV + hd * 256, 256)])
                wpush([(wl, 0, 16, RG + hd * 256, 256)])
            for hd in range(8):
                P.barrier()
                RTR.reset()
                TMP.reset()
                QT = RTR.alloc([128, T], BF16); QTK = Trk()
                KT = RTR.alloc([128, T], BF16); KTK = Trk()
                VT = RTR.alloc([128, 8, 256], BF16); VTK = Trk()
                SGT = RTR.alloc([128, 8, 256], BF16); SGK = Trk()
                KZF = RTR.alloc([128, 8, 128], BF16); KZFK = Trk()
                KZB = RTR.alloc([128, 8, 128], BF16); KZBK = Trk()
                SFB = RTR.alloc([128, 8, 256], BF16); SFK = Trk()
                SBB = RTR.alloc([128, 8, 256], BF16); SBK = Trk()
                SF = RTR.alloc([128, 256], F32); SFFK = dtrk('sf')
                SB = RTR.alloc([128, 256], F32); SBFK = dtrk('sb')
                MT = RTR.alloc([128, 128], F32); MTK = Trk()
                EA = RTR.alloc([128, 128], F32); EAK = Trk()
                PTb = [(RTR.alloc([128, 128], BF16), Trk()) for _ in range(2)]
                O1 = [(TMP.alloc([128, 256], F32), Trk()) for _ in range(2)]
                RTk = [(TMP.alloc([128, 256], BF16), Trk()) for _ in range(2)]
                JNK = TMP.alloc([128, 256], F32); JNKK = Trk()
                sm = TMP.alloc([128, 8, 4], F32); smk = Trk()
                XF = [(TMP.alloc([128, 512], F32), Trk()) for _ in range(2)]
                T1 = [(TMP.alloc([128, 512], F32), Trk()) for _ in range(2)]
                T2 = [(TMP.alloc([128, 512], F32), Trk()) for _ in range(2)]
                V(lambda: nc.vector.tensor_scalar(out=SMALL[:, 0:1], in0=NLG[:, l, hd:hd + 1], scalar1=-1.0, scalar2=None, op0=ALU.mult), r=(TAB,), w=(smk,))
                V(lambda: nc.vector.tensor_scalar(out=SMALL[:, 1:2], in0=NLG[:, l, 8 + hd:9 + hd], scalar1=-1.0, scalar2=None, op0=ALU.mult), r=(TAB,), w=(smk,))
                A(lambda: nc.scalar.activation(out=EA[:], in_=CF[:, C_DP:C_DP + 128], func=AF.Exp, scale=SMALL[:, 0:1]), r=(CST, smk), w=(EAK,))
                V(lambda: nc.vector.tensor_tensor(out=MT[:], in0=EA[:], in1=CF[:, C_MGE:C_MGE + 128], op=ALU.mult), r=(EAK, CST), w=(MTK,))
                A(lambda: nc.scalar.activation(out=EA[:], in_=CF[:, C_DN:C_DN + 128], func=AF.Exp, scale=SMALL[:, 1:2]), r=(CST, smk, MTK), w=(EAK,))
                V(lambda: nc.vector.tensor_tensor(out=EA[:], in0=EA[:], in1=CF[:, C_MLE:C_MLE + 128], op=ALU.mult), r=(EAK, CST), w=(EAK,))
                V(lambda: nc.vector.tensor_tensor(out=MT[:], in0=MT[:], in1=EA[:], op=ALU.add), r=(EAK, MTK), w=(MTK,))
                (wq_, wk_), wt = wtake()
                n = 0
                for (wv, dstT, dstK) in ((wq_, QT, QTK), (wk_, KT, KTK)):
                    for half in range(2):
                        tk = slice(half * 512, half * 512 + 512)
                        ps, psk = nps()
                        for k in range(16):
                            MMs(ps[:], psk, wv[:, k, :], HTt[:, k, tk], k == 0, k == 15, wt + (HT[k],))
                        if seg == 0:
                            A(lambda: nc.scalar.copy(out=dstT[:, tk], in_=ps[:]), r=(psk,), w=(dstK,))
                        else:
                            xf, xfk = XF[n % 2]
                            t1, t1k = T1[n % 2]
                            t2, t2k = T2[n % 2]
                            n += 1
                            A(lambda: nc.scalar.copy(out=xf[:], in_=ps[:]), r=(psk,), w=(xfk,))
                            p2, p2k = nps()
                            MMs(p2[:], p2k, PERM, xf[:], True, True, (xfk, CST))
                            r0 = half * 8
                            for (pl, ph, cos_ap, sin_ap) in (
                                (0, 64, CF[0:64, C_CR + r0:C_CR + r0 + 8].unsqueeze(2).broadcast_to([64, 8, 64]),
                                 CF[0:64, C_SR + r0:C_SR + r0 + 8].unsqueeze(2).broadcast_to([64, 8, 64])),
                                (64, 128, CF[64:128, C_CR:C_CR + 64].unsqueeze(1).broadcast_to([64, 8, 64]),
                                 CF[64:128, C_SR:C_SR + 64].unsqueeze(1).broadcast_to([64, 8, 64]))):
                                v3 = lambda ap: ap.rearrange("p (r c) -> p r c", c=64)
                                V(lambda: nc.vector.tensor_tensor(out=v3(t1[pl:ph, :]), in0=v3(xf[pl:ph, :]), in1=cos_ap, op=ALU.mult), r=(xfk, CST), w=(t1k,))
                                V(lambda: nc.vector.tensor_tensor(out=v3(t2[pl:ph, :]), in0=v3(p2[pl:ph, :]), in1=sin_ap, op=ALU.mult), r=(p2k, CST), w=(t2k,))
                            V(lambda: nc.vector.tensor_tensor(out=dstT[:, tk], in0=t1[:], in1=t2[:], op=ALU.add), r=(t1k, t2k), w=(dstK,))
                (wv,), wt = wtake()
                proj_tm(wv, wt, 256, lambda tt, ps, psk: A(lambda: nc.scalar.copy(out=VT[:, tt, :], in_=ps[:, 0:256]), r=(psk,), w=(VTK,)))
                (wv,), wt = wtake()
                proj_tm(wv, wt, 256, lambda tt, ps, psk: A(lambda: nc.scalar.activation(out=SGT[:, tt, :], in_=ps[:, 0:256], func=AF.Silu), r=(psk,), w=(SGK,)))
                for tt in range(8):
                    ps, psk = nps()
                    psb = ps[:, :].bitcast(BF16)
                    TR(psb[:, 0:128], psk, KT[:, tt * 128:(tt + 1) * 128], IDB, (KTK, CST))
                    V(lambda: nc.vector.tensor_scalar(out=KZF[:, tt, :], in0=psb[:, 0:128], scalar1=RTAB[:, 0, hd:hd + 1], scalar2=None, op0=ALU.mult), r=(psk, TBK), w=(KZFK,))
                    V(lambda: nc.vector.tensor_scalar(out=KZB[:, tt, :], in0=psb[:, 0:128], scalar1=RTAB[:, 0, 8 + hd:9 + hd], scalar2=None, op0=ALU.mult), r=(psk, TBK), w=(KZBK,))
                for s in range(nseq):
                    chunks = list(range(s * nchs, (s + 1) * nchs))
                    for (dr, S_, S_K, KZ, KZK, SXB, SXK, order) in ((0, SF, SFFK, KZF, KZFK, SFB, SFK, chunks), (1, SB, SBFK, KZB, KZBK, SBB, SBK, chunks[::-1])):
                        if seg == 1:
                            P.dma("sp", S_[:], sr[l, dr, hd], S_K, w=(S_K,))
                        else:
                            V(lambda: nc.vector.memset(S_[:], 0.0), w=(S_K,))
                        for n_ in order:
                            A(lambda: nc.scalar.copy(out=SXB[:, n_, :], in_=S_[:]), r=(S_K,), w=(SXK,))
                            ps, psk = nps()
                            MMs(ps[:, 0:256], psk, KZ[:, n_, :], VT[:, n_, :], True, True, (KZK, VTK))
                            V(lambda: nc.vector.scalar_tensor_tensor(out=S_[:], in0=S_[:], scalar=RTAB[:, 2, dr * 8 + hd:dr * 8 + hd + 1], in1=ps[:, 0:256], op0=ALU.mult, op1=ALU.add),
                              r=(S_K, psk, TBK), w=(S_K,))
                        if seg == 0:
                            P.dma("sp", ns[s, l, dr, hd], S_[:], S_K, r=(S_K,))
                for n_ in range(nseq * nchs):
                    ck_ = slice(n_ * 128, (n_ + 1) * 128)
                    ps, psk = nps()
                    MMs(ps[:, 0:128], psk, KT[:, ck_], QT[:, ck_], True, True, (KTK, QTK))
                    pt, ptk = PTb[n_ % 2]
                    V(lambda: nc.vector.tensor_tensor(out=pt[:], in0=ps[:, 0:128], in1=MT[:], op=ALU.mult), r=(psk, MTK), w=(ptk,))
                    po, pok = nps()
                    pc, pck = nps()
                    MMs(po[:, 0:256], pok, pt[:], VT[:, n_, :], True, True, (ptk, VTK))
                    MMs(po[:, 256:512], pok, QT[:, ck_], SFB[:, n_, :], True, True, (QTK, SFK))
                    MMs(pc[:, 0:256], pck, QT[:, ck_], SBB[:, n_, :], True, True, (QTK, SBK))
                    o1, o1k = O1[n_ % 2]
                    V(lambda: nc.vector.tensor_scalar(out=o1[:], in0=po[:, 256:512], scalar1=RTAB[:, 1, hd:hd + 1], scalar2=None, op0=ALU.mult), r=(pok, TBK), w=(o1k,))
                    V(lambda: nc.vector.scalar_tensor_tensor(out=o1[:], in0=pc[:, 0:256], scalar=RTAB[:, 1, 8 + hd:9 + hd], in1=o1[:], op0=ALU.mult, op1=ALU.add), r=(pck, TBK, o1k), w=(o1k,))
                    V(lambda: nc.vector.tensor_tensor(out=o1[:], in0=po[:, 0:256], in1=o1[:], op=ALU.add), r=(pok, o1k), w=(o1k,))
                    j4 = n_ % 8
                    A(lambda: nc.scalar.activation(out=JNK[:], in_=o1[:], func=AF.Square, accum_out=sm[:, j4, 0:1]), r=(o1k,), w=(JNKK, smk))
                    V(lambda: nc.vector.tensor_scalar(out=sm[:, j4, 1:2], in0=sm[:, j4, 0:1], scalar1=1.0 / 256, scalar2=EPS, op0=ALU.mult, op1=ALU.add), r=(smk,), w=(smk,))
                    A(lambda: nc.scalar.activation(out=sm[:, j4, 2:3], in_=sm[:, j4, 1:2], func=AF.Sqrt), r=(smk,), w=(smk,))
                    V(lambda: nc.vector.reciprocal(out=sm[:, j4, 2:3], in_=sm[:, j4, 2:3]), r=(smk,), w=(smk,))
                    rt, rtk = RTk[n_ % 2]
                    V(lambda: nc.vector.scalar_tensor_tensor(out=rt[:], in0=o1[:], scalar=sm[:, j4, 2:3], in1=SGT[:, n_, :], op0=ALU.mult, op1=ALU.mult), r=(o1k, smk, SGK), w=(rtk,))
                    p3, p3k = nps()
                    p3b = p3[:, :].bitcast(BF16)
                    for e2 in range(2):
                        TR(p3b[:, e2 * 128:(e2 + 1) * 128], p3k, rt[:, e2 * 128:(e2 + 1) * 128], IDB, (rtk, CST))
                    A(lambda: nc.scalar.copy(out=BRt[:, hd * 2:hd * 2 + 2, ck_], in_=p3b[:, 0:256].rearrange("p (e t) -> p e t", t=128)), r=(p3k,), w=(BR[hd * 2], BR[hd * 2 + 1]))

        def gmlp(l):
            wl = w_in[l]
            for s in range(4):
                wpush([(wl, 0, 16, GV + s * 256, 256)])
            for g in range(4):
                wpush([(wl, 0, 16, GU + g * 256, 256)])
            BRU.reset(pool=True)
            TMP.reset()
            VTK_ = BRU.alloc([128, 8, 1024], BF16); VK = [Trk() for _ in range(8)]
            GNB = TMP.alloc([128, 1024], F32); GK = dtrk('gk')
            WSN = TMP.alloc([128, 4, 128], BF16)
            WST = TMP.alloc([128, 4, 128], BF16); WK = dtrk('wk')
            BS = TMP.alloc([128, 4], F32)
            ST = TMP.alloc([128, 8, 4, 6], F32); STK = Trk()
            MV = TMP.alloc([128, 8, 4], F32); MVK = Trk()
            UB = [(TMP.alloc([128, 256], BF16), Trk()) for _ in range(2)]
            GTb = [(TMP.alloc([128, 256], BF16), Trk()) for _ in range(2)]
            P.dma("sp", GNB[:], gm_norm[l:l + 1, :].broadcast_to([128, 1024]), GK, w=(GK,), nonc=True)
            P.dma("pool", WSN[:], gm_ws[l].rearrange("g i j -> i g j"), WK, w=(WK,))
            P.dma("sp", BS[:], gm_bs[l].rearrange("g i -> i g"), GK, w=(GK,), nonc=True)
            ps, psk = nps()
            psb = ps[:, :].bitcast(BF16)
            for g in range(4):
                TR(psb[:, g * 128:(g + 1) * 128], psk, WSN[:, g, :], IDB, (WK, CST))
            WTK = Trk()
            A(lambda: nc.scalar.copy(out=WST[:].rearrange("p g i -> p (g i)"), in_=psb[:, 0:512]), r=(psk,), w=(WTK,))
            for s in range(4):
                (wv,), wt = wtake()

                def ev(tt, ps, psk, s=s):
                    V(lambda: nc.vector.tensor_copy(out=VTK_[:, tt, s * 256:(s + 1) * 256], in_=ps[:, 0:256]), r=(psk,), w=(VK[tt],))
                    V(lambda: nc.vector.bn_stats(out=ST[:, tt, s, :], in_=ps[:, 0:256]), r=(psk,), w=(STK,))
                proj_tm(wv, wt, 256, ev)
            for tt in range(8):
                V(lambda: nc.vector.bn_aggr(out=MV[:, tt, 0:2], in_=ST[:, tt].rearrange("p s x -> p (s x)")), r=(STK,), w=(MVK,))
                V(lambda: nc.vector.tensor_scalar(out=MV[:, tt, 2:3], in0=MV[:, tt, 1:2], scalar1=EPS, scalar2=None, op0=ALU.add), r=(MVK,), w=(MVK,))
                rsq(MV[:, tt, 2:3], MVK)
                V(lambda: nc.vector.tensor_scalar(out=VTK_[:, tt, :], in0=VTK_[:, tt, :], scalar1=MV[:, tt, 0:1], scalar2=MV[:, tt, 2:3], op0=ALU.subtract, op1=ALU.mult), r=(MVK, VK[tt]), w=(VK[tt],))
                V(lambda: nc.vector.tensor_tensor(out=VTK_[:, tt, :], in0=VTK_[:, tt, :], in1=GNB[:], op=ALU.mult), r=(VK[tt], GK), w=(VK[tt],))
            n = 0
            for g in range(4):
                (wv,), wt = wtake()

                def eu(tt, ps, psk, g=g):
                    i = n_[0] % 2
                    n_[0] += 1
                    ub, ubk = UB[i]
                    gt, gtk = GTb[i]
                    A(lambda: nc.scalar.copy(out=ub[:], in_=ps[:, 0:256]), r=(psk,), w=(ubk,))
                    pm, pmk = nps()
                    MMs(pm[:, 0:256], pmk, WST[:, g, :], VTK_[:, tt, g * 256:(g + 1) * 256], True, True, (WTK, VK[tt]))
                    V(lambda: nc.vector.scalar_tensor_tensor(out=gt[:], in0=pm[:, 0:256], scalar=BS[:, g:g + 1], in1=ub[:], op0=ALU.add, op1=ALU.mult), r=(pmk, GK, ubk), w=(gtk,))
                    p3, p3k = nps()
                    p3b = p3[:, :].bitcast(BF16)
                    for e2 in range(2):
                        TR(p3b[:, e2 * 128:(e2 + 1) * 128], p3k, gt[:, e2 * 128:(e2 + 1) * 128], IDB, (gtk, CST))
                    A(lambda: nc.scalar.copy(out=BRt[:, g * 2:g * 2 + 2, tt * 128:(tt + 1) * 128], in_=p3b[:, 0:256].rearrange("p (e t) -> p e t", t=128)), r=(p3k,), w=(BR[g * 2], BR[g * 2 + 1]))
                n_ = [0]
                proj_tm(wv, wt, 256, eu)

        def attention(l, seg):
            wl = w_in[l]
            BRU.reset(pool=True)
            TMP.reset()
            psn[0] = 5
            psi[0] = 0
            sc = 128.0 ** -0.5
            if seg == 0:
                for s in range(4):
                    wpush([(wl, 0, 16, NK + s * 256, 256)])
                for s in range(4):
                    wpush([(wl, 0, 16, NV + s * 256, 256)])
                for hd in range(8):
                    wpush([(wl, 0, 16, NQ + hd * 128, 128), (wl, 0, 16, NK + hd * 128, 128)])
                VTOK = BRU.alloc([128, 8, 1024], BF16); VTK = Trk()
                STG = [(TMP.alloc([128, 256], F32), dtrk('kv%d' % _)) for _ in range(2)]
                QT = TMP.alloc([128, T], BF16); QTK = Trk()
                KT = TMP.alloc([128, T], BF16); KTK = Trk()
                SALL = TMP.alloc([128, 256], F32); SALLK = Trk()
                PB = TMP.alloc([128, 256], BF16); PBK = Trk()
                PTT = TMP.alloc([128, 256], BF16); PTK = Trk()
                sm = TMP.alloc([128, 4], F32); smk = Trk()
                cnt = [0]
                for (isv, dst) in ((0, nk), (1, nv)):
                    for s in range(4):
                        (wv,), wt = wtake()

                        def ev(tt, ps, psk, s=s, isv=isv, dst=dst):
                            sg, sgk = STG[cnt[0] % 2]
                            cnt[0] += 1
                            V(lambda: nc.vector.tensor_copy(out=sg[:], in_=ps[:, 0:256]), r=(psk,), w=(sgk,))
                            if isv:
                                A(lambda: nc.scalar.copy(out=VTOK[:, tt, s * 256:(s + 1) * 256], in_=sg[:]), r=(sgk,), w=(VTK,))
                            b, t2 = tt // 2, tt % 2
                            P.dma("sp", dst[b, l, t2 * 128:(t2 + 1) * 128, s * 256:(s + 1) * 256], sg[:], sgk, r=(sgk,))
                        proj_tm(wv, wt, 256, ev)
                for hd in range(8):
                    (wq_, wk_), wt = wtake()
                    proj_fm(wq_, wt, 1, lambda cb, half, ps, psk: A(lambda: nc.scalar.activation(out=QT[:, half * 512:(half + 1) * 512], in_=ps[:], func=AF.Copy, scale=sc), r=(psk,), w=(QTK,)))
                    proj_fm(wk_, wt, 1, lambda cb, half, ps, psk: A(lambda: nc.scalar.copy(out=KT[:, half * 512:(half + 1) * 512], in_=ps[:]), r=(psk,), w=(KTK,)))
                    for half in range(2):
                        po, pok = npo()
                        for i4 in range(4):
                            s = half * 2 + i4 // 2
                            qt = i4 % 2
                            q0 = s * 256 + qt * 128
                            attn_core(128, QT[:, q0:q0 + 128], QTK, [(KT[:, s * 256:(s + 1) * 256], 256, KTK, None, None)],
                                      [(VTOK[:, s * 2 + t, hd * 128:(hd + 1) * 128], VTK) for t in range(2)],
                                      po[:, i4 * 128:(i4 + 1) * 128], pok, None, 256, SALL, SALLK, PB, PBK, PTT, PTK, sm, smk)
                        A(lambda: nc.scalar.copy(out=BRt[:, hd, half * 512:(half + 1) * 512], in_=po[:]), r=(pok,), w=(BR[hd],))
                psn[0] = 7
                psi[0] = 0
            else:
                for hd in range(8):
                    wpush([(wl, 0, 16, NQ + hd * 128, 128), (wl, 0, 16, NK + hd * 128, 128)])
                    wpush([(wl, 0, 16, NV + hd * 128, 128)])
                EH = BRU.alloc([128, 4096], BF16); EHK = dtrk('eh')
                CKt = BRU.alloc([128, 2, 1024], BF16); CKK = dtrk('ck')
                CVt = BRU.alloc([128, 2, 1024], BF16); CVK = dtrk('cv')
                P.dma("sp", EH[0:32, :], ehot, EHK, w=(EHK,))
                P.dma("pool", CKt[:], ck[l].rearrange("(t p) f -> p t f", p=128), CKK, w=(CKK,))
                P.dma("pool", CVt[:], cv[l].rearrange("(t p) f -> p t f", p=128), CVK, w=(CVK,))
                RBF = TMP.alloc([128, 8, 15], F32); RBK = dtrk('rb')
                RBB = TMP.alloc([128, 8, 16], BF16); RBBK = Trk()
                P.dma("sp", RBF[0:31], na_rpb[l].rearrange("h r c -> c h r"), RBK, w=(RBK,), nonc=True)
                V(lambda: nc.vector.memset(RBB[0:32], 0.0), w=(RBBK,))
                V(lambda: nc.vector.tensor_copy(out=RBB[0:31, :, 0:15], in_=RBF[0:31]), r=(RBK,), w=(RBBK,))
                QT = TMP.alloc([128, T], BF16); QTK = Trk()
                KT = TMP.alloc([128, T], BF16); KTK = Trk()
                VTt = TMP.alloc([128, T], BF16); VTTK = Trk()
                VA = TMP.alloc([128, 8, 128], BF16); VAK = Trk()
                VB = TMP.alloc([128, 7, 128], BF16); VBK = Trk()
                KCT = TMP.alloc([128, 256], BF16); KCK = Trk()
                BH = TMP.alloc([64, 64, 16], F32); BHK = Trk()
                SALL = TMP.alloc([64, 768], F32); SALLK = Trk()
                PB = TMP.alloc([64, 768], BF16); PBK = Trk()
                PTT = TMP.alloc([128, 384], BF16); PTK = Trk()
                sm = TMP.alloc([64, 4], F32); smk = Trk()
                for hd in range(8):
                    (wq_, wk_), wt = wtake()
                    proj_fm(wq_, wt, 1, lambda cb, half, ps, psk: A(lambda: nc.scalar.activation(out=QT[:, half * 512:(half + 1) * 512], in_=ps[:], func=AF.Copy, scale=sc), r=(psk,), w=(QTK,)))
                    proj_fm(wk_, wt, 1, lambda cb, half, ps, psk: A(lambda: nc.scalar.copy(out=KT[:, half * 512:(half + 1) * 512], in_=ps[:]), r=(psk,), w=(KTK,)))
                    (wv_,), wt = wtake()
                    proj_fm(wv_, wt, 1, lambda cb, half, ps, psk: A(lambda: nc.scalar.copy(out=VTt[:, half * 512:(half + 1) * 512], in_=ps[:]), r=(psk,), w=(VTTK,)))
                    for (dstV, dstK, ntile, toff) in ((VA, VAK, 8, 0), (VB, VBK, 7, 64)):
                        for t0 in range(0, ntile, 4):
                            ps, psk = nps()
                            psb = ps[:, :].bitcast(BF16)
                            nn = min(4, ntile - t0)
                            for t in range(nn):
                                TR(psb[:, t * 128:(t + 1) * 128], psk, VTt[:, toff + (t0 + t) * 128: toff + (t0 + t + 1) * 128], IDB, (VTTK, CST))
                            A(lambda: nc.scalar.copy(out=dstV[:, t0:t0 + nn, :].rearrange("p t d -> p (t d)"), in_=psb[:, 0:nn * 128]), r=(psk,), w=(dstK,))
                    ps, psk = nps()
                    psb = ps[:, :].bitcast(BF16)
                    for t in range(2):
                        TR(psb[:, t * 128:(t + 1) * 128], psk, CKt[:, t, hd * 128:(hd + 1) * 128], IDB, (CKK, CST))
                    A(lambda: nc.scalar.copy(out=KCT[:], in_=psb[:, 0:256]), r=(psk,), w=(KCK,))
                    pB0, pB0k = nps()
                    pB1, pB1k = nps()
                    for kc in range(64):
                        pB, pBk = (pB0, pB0k) if kc < 32 else (pB1, pB1k)
                        o = (kc % 32) * 16
                        MMs(pB[0:64, o:o + 16], pBk, EH[0:32, kc * 64:(kc + 1) * 64], RBB[0:32, hd, 0:16], True, True, (EHK, RBBK))
                    for hb, (pB, pBk) in enumerate(((pB0, pB0k), (pB1, pB1k))):
                        V(lambda: nc.vector.tensor_tensor(out=BH[:, hb * 32:(hb + 1) * 32, 0:15], in0=pB[0:64, :].rearrange("p (k r) -> p k r", r=16)[:, :, 0:15],
                                                          in1=CF[0:64, C_NEGM + hb * 32:C_NEGM + (hb + 1) * 32].unsqueeze(2).broadcast_to([64, 32, 15]), op=ALU.add),
                          r=(pBk, CST), w=(BHK,))
                    for rh in range(2):
                        po, pok = npo()
                        for r8 in range(8):
                            r_ = rh * 8 + r8
                            st = min(max(r_ - 4, 0), 8)
                            ro0 = st - r_ + 7
                            bias = BH[:, :, ro0:ro0 + 8].rearrange("p k j -> p j k")
                            if st % 2 == 0:
                                vt = [(VA[:, st // 2 + t, :], VAK) for t in range(4)]
                            else:
                                vt = [(VB[:, (st - 1) // 2 + t, :], VBK) for t in range(4)]
                            vt += [(CVt[:, t, hd * 128:(hd + 1) * 128], CVK) for t in range(2)]
                            attn_core(64, QT[:, r_ * 64:(r_ + 1) * 64], QTK,
                                      [(KT[:, st * 64:st * 64 + 512], 512, KTK, bias, BHK), (KCT[:], 256, KCK, None, None)],
                                      vt, po[:, r8 * 64:(r8 + 1) * 64], pok, None, 768, SALL, SALLK, PB, PBK, PTT, PTK, sm, smk)
                        A(lambda: nc.scalar.copy(out=BRt[:, hd, rh * 512:(rh + 1) * 512], in_=po[:]), r=(pok,), w=(BR[hd],))
            psn[0] = 7
            psi[0] = 0

        def load_x(seg):
            P.barrier()
            TMP.reset()
            STG = [(TMP.alloc([128, D], F32), dtrk('xs%d' % _)) for _ in range(2)]
            for tt in range(8):
                sg, sgk = STG[tt % 2]
                P.dma("sp", sg[:], xin[seg][tt * 128:(tt + 1) * 128, :], sgk, w=(sgk,))
                for c0 in range(0, NCH, 4):
                    ps, psk = nps()
                    for c in range(4):
                        TR(ps[:, c * 128:(c + 1) * 128], psk, sg[:, (c0 + c) * 128:(c0 + c + 1) * 128], IDF, (sgk, CST))
                    V(lambda: nc.vector.tensor_copy(out=XTt[:, c0:c0 + 4, tt * 128:(tt + 1) * 128], in_=ps[:].rearrange("p (c t) -> p c t", t=128)),
                      r=(psk,), w=tuple(XT[c0:c0 + 4]))

        def store_x(seg):
            P.barrier()
            TMP.reset()
            STG = [(TMP.alloc([128, D], F32), dtrk('xs%d' % _)) for _ in range(2)]
            for tt in range(8):
                sg, sgk = STG[tt % 2]
                for c0 in range(0, NCH, 4):
                    ps, psk = nps()
                    for c in range(4):
                        TR(ps[:, c * 128:(c + 1) * 128], psk, XTt[:, c0 + c, tt * 128:(tt + 1) * 128], IDF, (XT[c0 + c], CST))
                    V(lambda: nc.vector.tensor_copy(out=sg[:, c0 * 128:(c0 + 4) * 128], in_=ps[:]), r=(psk,), w=(sgk,))
                P.dma("sp", yout[seg][tt * 128:(tt + 1) * 128, :], sg[:], sgk, r=(sgk,))

        stages = os.environ.get("KSTAGES", "all")
        stopped = False
        for seg in [int(c_) for c_ in os.environ.get("KSEGS", "01")]:
            load_x(seg)
            try:
                for l in range(2):
                    if stages in ("all", "ffn"):
                        ffn(l, 0, seg)
                        stop_at("f0")
                    if stages in ("all", "mix"):
                        mixer(l, seg)
                        stop_at("m0")
                    if stages in ("all", "ffn"):
                        ffn(l, 1, seg)
                    stop_at("l0")
                stop_at("s0")
            except Stop:
                stopped = True
            store_x(seg)
            if stopped:
                break
        if not stopped:
            assert wstate["taken"] == len(wq), (wstate, len(wq))
        P.finish()
        print("instructions:", P.nins, "sems:", len(P.sems), "dma_tot:", P.dma_tot, "eng:", {k: v["count"] for k, v in P.eng.items()})
    return nc


def _consts():
    cf = np.zeros((128, NCF), np.float32)
    cf[:, C_IDF:C_IDF + 128] = np.eye(128, dtype=np.float32)
    pm = np.zeros((128, 128), np.float32)
    for m in range(128):
        blk = m // 32
        partner = m + 32 if blk % 2 == 0 else m - 32
        pm[partner, m] = 1.0
    cf[:, C_PERM:C_PERM + 128] = pm
    inv = (10000.0 ** (-np.arange(0, 64, 2, dtype=np.float32) / 64.0)).astype(np.float32)
    for d in range(128):
        f = d % 32
        sgn = -1.0 if (d % 64) < 32 else 1.0
        npos = 16 if d < 64 else 64
        pos = np.arange(npos, dtype=np.float32)
        ang = (pos * inv[f]).astype(np.float32)
        cf[d, C_CR:C_CR + npos] = np.cos(ang)
        cf[d, C_SR:C_SR + npos] = sgn * np.sin(ang)
    j = np.arange(128)[:, None].astype(np.float32)
    i = np.arange(128)[None, :].astype(np.float32)
    dd = i - j
    cf[:, C_DP:C_DP + 128] = np.maximum(dd, 0)
    cf[:, C_DN:C_DN + 128] = np.maximum(-dd, 0)
    cf[:, C_MGE:C_MGE + 128] = SC_RET * (dd >= 0)
    cf[:, C_MLE:C_MLE + 128] = SC_RET * (dd <= 0)
    q = np.arange(64)
    cs = np.clip(q - 8, 0, 48)
    kc = np.arange(64)
    valid = (kc[None, :] >= cs[:, None]) & (kc[None, :] < cs[:, None] + 16)
    negm = np.where(valid, 0.0, NEGB).astype(np.float32)
    cf[0:64, C_NEGM:C_NEGM + 64] = negm
    cf[64:128, C_NEGM:C_NEGM + 64] = negm
    p = np.arange(128, dtype=np.float32)
    cf[:, C_POS + 0] = 127 - p
    cf[:, C_POS + 1] = p
    cf[:, C_POS + 2] = p + 1
    cf[:, C_POS + 3] = 128 - p
    cb = np.zeros((128, 256), np.float32)
    cb[:, 0:128] = np.eye(128)
    cb[:, 128:256] = 1.0
    cb = cb.astype(ml_dtypes.bfloat16)
    eh = np.zeros((31, 64, 64), np.float32)
    for kcc in range(64):
        for qq in range(64):
            ci = kcc - qq + 15
            if 0 <= ci <= 30:
                eh[ci, kcc, qq] = 1.0
    eh = np.concatenate([eh.reshape(31, 4096), np.zeros((1, 4096), np.float32)], axis=0).astype(ml_dtypes.bfloat16)
    return cf, cb, eh


_CACHE = {}


def kernel(x_prompt, x_sample, cache_k, cache_v, state_ret, c, c_ctx, w_ada, b_ada, norm_pre, norm_post,
           ffn_w_in, ffn_w_out, w_in, ret_decay_logit, gm_norm, gm_ws, gm_bs, na_rpb,
           w_branch_ret, w_branch_gm, w_branch_na, w_out):
    f = lambda a: np.ascontiguousarray(np.asarray(a), dtype=np.float32)
    dbg = os.environ.get("KDBG")
    if "nc" not in _CACHE:
        _CACHE["nc"] = build_program(dbg)
    nc = _CACHE["nc"]
    cf, cb, eh = _consts()
    shared = dict(w_ada=f(w_ada), b_ada=f(b_ada), norm_pre=f(norm_pre), norm_post=f(norm_post), ffn_w_in=f(ffn_w_in),
                  ffn_w_out=f(ffn_w_out), w_in=f(w_in), ret_decay_logit=f(ret_decay_logit).reshape(2, 16), gm_norm=f(gm_norm),
                  gm_ws=f(gm_ws), gm_bs=f(gm_bs), na_rpb=f(na_rpb), w_branch_ret=f(w_branch_ret), w_branch_gm=f(w_branch_gm),
                  w_branch_na=f(w_branch_na), w_out=f(w_out), cstf=cf, cstb=cb, ehot=eh)
    xp, xs, ckk, cvv, srr, cc, cx = f(x_prompt), f(x_sample), f(cache_k), f(cache_v), f(state_ret), f(c), f(c_ctx)
    in_maps = []
    for i in range(8):
        m = dict(shared)
        m["xp"] = xp[4 * i:4 * i + 4].reshape(T, D)
        m["xs"] = xs[i]
        m["ck"] = ckk[i].reshape(2, 256, 1024)
        m["cv"] = cvv[i].reshape(2, 256, 1024)
        m["sr"] = srr[i]
        m["c2"] = np.stack([cx, cc[i]], axis=0)
        in_maps.append(m)
    res = run_bass_kernel_spmd(nc, in_maps, core_ids=list(range(8)))
    R = res.results
    y_prompt = np.concatenate([R[i]["yp"].reshape(4, 256, D) for i in range(8)], axis=0)
    y_sample = np.stack([R[i]["ys"] for i in range(8)], axis=0)
    nkk = np.concatenate([R[i]["nk"].reshape(4, 2, 256, 8, 128) for i in range(8)], axis=0)
    nvv = np.concatenate([R[i]["nv"].reshape(4, 2, 256, 8, 128) for i in range(8)], axis=0)
    nss = np.concatenate([R[i]["ns"] for i in range(8)], axis=0)
    return (y_prompt.astype(np.float32), y_sample.astype(np.float32), nkk.astype(np.float32), nvv.astype(np.float32), nss.astype(np.float32))
```

```python
import os
import numpy as np
import ml_dtypes
from contextlib import ExitStack
import concourse.bass as bass
import concourse.mybir as mybir
from concourse.bass_utils import run_bass_kernel_spmd

F32 = mybir.dt.float32
BF16 = mybir.dt.bfloat16
AF = mybir.ActivationFunctionType
ALU = mybir.AluOpType
AX = mybir.AxisListType

D = 2048
T = 1024
NCH = 16
FF = 5632
NF = 44
INW = 17408
EPS = 1e-6
RQ, RK, RV, RG, GU, GV, NQ, NK, NV, GA, GB, GC = 0, 1024, 2048, 4096, 6144, 7168, 8192, 9216, 10240, 11264, 13312, 15360
SC_RET = 128.0 ** -0.5
NEGB = -30000.0

C_IDF, C_PERM, C_CR, C_SR, C_DP, C_DN, C_MGE, C_MLE, C_NEGM, C_POS = 0, 128, 256, 320, 384, 512, 640, 768, 896, 960
NCF = 964


class Trk:
    __slots__ = ("w", "r", "sid", "dc")

    def __init__(self):
        self.w = {}
        self.r = {}
        self.sid = None
        self.dc = 0


class Prog:
    def __init__(self, nc, es):
        self.nc = nc
        self.es = es
        self.sems = []
        self.eng = {}
        for name, e in (("pe", nc.tensor), ("act", nc.scalar), ("dve", nc.vector), ("pool", nc.gpsimd), ("sp", nc.sync)):
            sid = self.newsem(name)
            self.eng[name] = dict(e=e, sid=sid, count=0, seen={})
        self.dma_tot = {}
        self.uid = 0
        self.nins = 0

    def newsem(self, name):
        h = self.es.enter_context(self.nc.semaphore("s%s%d" % (name, len(self.sems))))
        self.sems.append(h)
        return len(self.sems) - 1

    def _need(self, E, r, w, skip=None):
        need = {}
        own = E["sid"]
        for b in r:
            for sid, v in b.w.items():
                if need.get(sid, 0) < v:
                    need[sid] = v
        for b in w:
            for sid, v in b.w.items():
                if sid != own and need.get(sid, 0) < v:
                    need[sid] = v
            for sid, v in b.r.items():
                if sid != own and need.get(sid, 0) < v:
                    need[sid] = v
        seen = E["seen"]
        for sid, v in need.items():
            if sid == skip:
                continue
            if seen.get(sid, 0) < v:
                E["e"].wait_ge(self.sems[sid], v)
                seen[sid] = v

    def op(self, en, fn, r=(), w=(), signal=True):
        E = self.eng[en]
        self._need(E, r, w)
        ins = fn()
        self.nins += 1
        if signal:
            ins.then_inc(self.sems[E["sid"]], 1)
            E["count"] += 1
            tok = E["count"]
        else:
            tok = E["count"] + 1
        sid = E["sid"]
        for b in r:
            if b.r.get(sid, 0) < tok:
                b.r[sid] = tok
        for b in w:
            b.w[sid] = tok
        return ins

    def dma(self, en, out, in_, semb, r=(), w=(), nonc=False):
        E = self.eng[en]
        if semb.sid is None or semb.dc >= 8000:
            semb.sid = self.newsem("d")
            semb.dc = 0
        self._need(E, r, w, skip=(semb.sid if w else None))
        if nonc:
            ins = E["e"].dma_start(out=out, in_=in_, allow_slow_non_contiguous=True)
        else:
            ins = E["e"].dma_start(out=out, in_=in_)
        self.nins += 1
        ins.then_inc(self.sems[semb.sid], 16)
        semb.dc += 16
        self.dma_tot[semb.sid] = semb.dc
        for b in r:
            if b.r.get(semb.sid, 0) < semb.dc:
                b.r[semb.sid] = semb.dc
        for b in w:
            b.w[semb.sid] = semb.dc
        return ins

    def barrier(self, names=("pe", "act", "dve", "sp")):
        for a in names:
            A = self.eng[a]
            for b in names:
                if a == b:
                    continue
                B = self.eng[b]
                if A["seen"].get(B["sid"], 0) < B["count"]:
                    A["e"].wait_ge(self.sems[B["sid"]], B["count"])
                    A["seen"][B["sid"]] = B["count"]

    def finish(self):
        sp = self.eng["sp"]
        for sid, tot in self.dma_tot.items():
            if sp["seen"].get(sid, 0) < tot:
                sp["e"].wait_ge(self.sems[sid], tot)
                sp["seen"][sid] = tot
        for n in ("pe", "act", "dve", "pool"):
            B = self.eng[n]
            if B["count"] > 0 and sp["seen"].get(B["sid"], 0) < B["count"]:
                sp["e"].wait_ge(self.sems[B["sid"]], B["count"])


class Stop(Exception):
    pass


KSTOP = os.environ.get("KSTOP", "")


def stop_at(name):
    if KSTOP == name:
        raise Stop()


class Region:
    def __init__(self, P, base, size):
        self.P = P
        self.base = base
        self.size = size
        self.off = 0

    def reset(self, pool=False):
        self.P.barrier(("pe", "act", "dve", "sp", "pool") if pool else ("pe", "act", "dve", "sp"))
        self.off = 0

    def alloc(self, shape, dt):
        n = 1
        for s in shape[1:]:
            n *= s
        nb = n * (4 if dt == F32 else 2)
        nb = (nb + 31) // 32 * 32
        assert self.off + nb <= self.size, ("region overflow", self.off, nb, self.size)
        self.P.uid += 1
        t = self.P.nc.alloc_sbuf_tensor_at("r%d" % self.P.uid, list(shape), dt, offset=self.base + self.off)
        self.off += nb
        return t


def build_program(dbg=None):
    nc = bass.Bass("TRN2", target_bir_lowering=False)

    def din(name, shape, dt=F32):
        return nc.dram_tensor(name, list(shape), dt, kind="ExternalInput").ap()

    def dout(name, shape):
        return nc.dram_tensor(name, list(shape), F32, kind="ExternalOutput").ap()

    xin = [din("xp", [T, D]), din("xs", [T, D])]
    ck = din("ck", [2, 256, 1024])
    cv = din("cv", [2, 256, 1024])
    sr = din("sr", [2, 2, 8, 128, 256])
    c2 = din("c2", [2, D])
    w_ada = din("w_ada", [2, D, 9 * D])
    b_ada = din("b_ada", [2, 9 * D])
    norm_pre = din("norm_pre", [2, 3, D])
    norm_post = din("norm_post", [2, 3, D])
    ffn_w_in = din("ffn_w_in", [2, 2, D, 2 * FF])
    ffn_w_out = din("ffn_w_out", [2, 2, FF, D])
    w_in = din("w_in", [2, D, INW])
    rdl = din("ret_decay_logit", [2, 16])
    gm_norm = din("gm_norm", [2, 1024])
    gm_ws = din("gm_ws", [2, 4, 128, 128])
    gm_bs = din("gm_bs", [2, 4, 128])
    na_rpb = din("na_rpb", [2, 8, 15, 31])
    w_br = din("w_branch_ret", [2, D, D])
    w_bg = din("w_branch_gm", [2, 1024, D])
    w_bn = din("w_branch_na", [2, 1024, D])
    w_o = din("w_out", [2, D, D])
    cstf = din("cstf", [128, NCF])
    cstb = din("cstb", [128, 256], BF16)
    ehot = din("ehot", [32, 4096], BF16)
    yout = [dout("yp", [T, D]), dout("ys", [T, D])]
    nk = dout("nk", [4, 2, 256, 1024])
    nv = dout("nv", [4, 2, 256, 1024])
    ns = dout("ns", [4, 2, 2, 8, 128, 256])
    dbg_out = dout("dbg", [128, NCH * T]) if dbg else None

    es = ExitStack()
    with es:
        arena = es.enter_context(nc.sbuf_tensor("arena", [128, 212000 // 4], F32))
        P = Prog(nc, es)
        SB0 = 16512
        XT_OFF, AR_OFF, TMP_OFF, WR_OFF, CST_OFF = SB0, SB0 + 65536, SB0 + 163840, SB0 + 186368, SB0 + 202752

        def at(name, shape, dt, off):
            P.uid += 1
            return nc.alloc_sbuf_tensor_at("%s%d" % (name, P.uid), list(shape), dt, offset=off)

        XTt = at("xt", [128, NCH, T], F32, XT_OFF)
        XT = [Trk() for _ in range(NCH)]
        HTt = at("ht", [128, NCH, T], BF16, AR_OFF)
        BRt = at("br", [128, NCH, T], BF16, AR_OFF + 32768)
        Mt = at("m", [128, NCH, T], BF16, AR_OFF + 65536)
        Yt = at("y", [128, NCH, T], F32, AR_OFF)
        HT = [Trk() for _ in range(NCH)]
        BR = [Trk() for _ in range(NCH)]
        MM = [Trk() for _ in range(NCH)]
        YY = [Trk() for _ in range(NCH)]
        HHt = at("hh", [128, NCH, 512], BF16, AR_OFF)
        Gt = at("g", [128, NF, 512], BF16, AR_OFF + 16384)
        YFt = at("yf", [128, NCH, 512], F32, AR_OFF + 61440)
        HH = [Trk() for _ in range(NCH)]
        GG = [Trk() for _ in range(NF)]
        YF = [Trk() for _ in range(NCH)]
        RTR = Region(P, AR_OFF + 65536, 32768)
        BRU = Region(P, AR_OFF + 32768 + 16384, 16384)
        TMP = Region(P, TMP_OFF, 22528)
        WSL = [at("wsl", [128, 4096], BF16, WR_OFF + 8192 * i) for i in range(2)]
        WTR = [(Trk(), Trk()) for _ in range(2)]
        co = [CST_OFF]

        def calloc(shape, dt):
            n = 1
            for s in shape[1:]:
                n *= s
            nb = (n * (4 if dt == F32 else 2) + 31) // 32 * 32
            t = at("c", shape, dt, co[0])
            co[0] += nb
            assert co[0] <= SB0 + 212000, co[0]
            return t

        CF = calloc([128, NCF], F32)
        CB = calloc([128, 256], BF16)
        MOD = calloc([128, 2, 9, NCH, 2], F32)
        NPRE = calloc([128, 6, NCH], F32)
        NPOST = calloc([128, 6, NCH], F32)
        NLG = calloc([128, 2, 16], F32)
        RTAB = calloc([128, 6, 16], F32)
        ACO = calloc([128, NCH], F32)
        COEF = calloc([128, NCH], F32)
        SMALL = calloc([128, 64], F32)
        CST = Trk()
        DT = {}

        def dtrk(name):
            if name not in DT:
                DT[name] = Trk()
            return DT[name]
        TAB = Trk()
        ACT_ = Trk()
        IDF = CF[:, C_IDF:C_IDF + 128]
        PERM = CF[:, C_PERM:C_PERM + 128]
        IDB = CB[:, 0:128]
        ONESB = CB[:, 128:256]

        PSt = [nc.alloc_psum_tensor("ps%d" % i, [128, 512], F32) for i in range(8)]
        PSK = [Trk() for _ in range(8)]
        psi = [0]
        psn = [7]
        pso = [0]

        def nps():
            i = psi[0] % psn[0]
            psi[0] = (i + 1) % psn[0]
            return PSt[i], PSK[i]

        def npo():
            i = 5 + pso[0]
            pso[0] ^= 1
            return PSt[i], PSK[i]

        PSS, PSSK = PSt[7], PSK[7]

        V = lambda fn, r=(), w=(): P.op("dve", fn, r, w)
        A = lambda fn, r=(), w=(): P.op("act", fn, r, w)

        def MMs(ps_ap, psk, lhsT, rhs, start, stop, r, sig=None):
            return P.op("pe", lambda: nc.tensor.matmul(ps_ap, lhsT=lhsT, rhs=rhs, start=start, stop=stop), r, (psk,), signal=(stop if sig is None else sig))

        def TR(ps_ap, psk, in_, ident, r):
            return P.op("pe", lambda: nc.tensor.transpose(ps_ap, in_, ident), r, (psk,))

        wq = []
        wstate = dict(issued=0, taken=0, ready=[])

        def wpush(parts):
            tot = sum(p[2] * p[4] for p in parts)
            assert tot <= 4096, tot
            wq.append(parts)
            wkick()

        def _wissue():
            i = wstate["issued"]
            parts = wq[i]
            slot = i % 2
            views = []
            lo, hi = WTR[slot]
            split = (len(parts) == 2 and all(p[2] * p[4] <= 2048 for p in parts))
            off = 0
            for pi, (dr, r0, nkk, c0, ncols) in enumerate(parts):
                if split:
                    off = pi * 2048
                dst = WSL[slot][:, off:off + nkk * ncols].rearrange("p (k c) -> p k c", c=ncols)
                src = dr[r0:r0 + nkk * 128, c0:c0 + ncols].rearrange("(k p) c -> p k c", p=128)
                if split:
                    tk_ = lo if pi == 0 else hi
                    P.dma("pool", dst, src, tk_, r=(), w=(tk_,))
                else:
                    P.dma("pool", dst, src, lo, r=(), w=(lo, hi))
                views.append(dst)
                off += nkk * ncols
            wstate["ready"].append((views, (lo, hi)))
            wstate["issued"] = i + 1

        def wkick():
            while wstate["issued"] < min(len(wq), wstate["taken"] + 2):
                _wissue()

        def wtake():
            n = wstate["taken"]
            while wstate["issued"] < min(len(wq), n + 2):
                _wissue()
            wstate["taken"] = n + 1
            return wstate["ready"][n]

        P.dma("sp", CF[:], cstf, CST, w=(CST,))
        P.dma("sp", CB[:], cstb, CST, w=(CST,))
        P.dma("sp", NPRE[:], norm_pre.rearrange("l s (c p) -> p (l s) c", p=128), CST, w=(CST,), nonc=True)
        P.dma("sp", NPOST[:], norm_post.rearrange("l s (c p) -> p (l s) c", p=128), CST, w=(CST,), nonc=True)
        TMP.reset()
        BADA = TMP.alloc([128, 2, 144], F32)
        SCF = TMP.alloc([128, 2, NCH], F32)
        SCB = TMP.alloc([128, NCH, 2], BF16)
        LGR = TMP.alloc([128, 32], F32)
        P.dma("sp", BADA[:], b_ada.rearrange("l (j p) -> p l j", p=128), CST, w=(CST,), nonc=True)
        for r_ in range(2):
            P.dma("sp", SCF[:, r_, :], c2[r_].rearrange("(c p) -> p c", p=128), CST, w=(CST,), nonc=True)
        P.dma("sp", LGR[:], rdl.rearrange("l x -> (l x)").rearrange("(o x) -> o x", o=1).broadcast_to([128, 32]), CST, w=(CST,), nonc=True)
        STP = Trk()
        A(lambda: nc.scalar.activation(out=SCB[:].rearrange("p c r -> p r c"), in_=SCF[:], func=AF.Silu), r=(CST,), w=(STP,))
        A(lambda: nc.scalar.activation(out=LGR[:], in_=LGR[:], func=AF.Exp, scale=-1.0), r=(CST,), w=(STP,))
        A(lambda: nc.scalar.activation(out=NLG[:].rearrange("p l x -> p (l x)"), in_=LGR[:], func=AF.Ln, bias=1.0), r=(STP,), w=(TAB,))

        for l in range(2):
            for jb in range(72):
                wpush([(w_ada[l], 0, 16, jb * 256, 256)])
        for l in range(2):
            pst, psk = nps()
            for jb in range(72):
                (wv,), wt = wtake()
                for jj in range(2):
                    j = jb * 2 + jj
                    for k in range(16):
                        MMs(pst[:, 2 * j:2 * j + 2], psk, wv[:, k, jj * 128:(jj + 1) * 128], SCB[:, k, :], k == 0, k == 15, wt + (STP,))
            V(lambda: nc.vector.tensor_tensor(out=MOD[:, l].rearrange("p a c k -> p (a c) k"),
                                              in0=pst[:, 0:288].rearrange("p (j k) -> p j k", k=2),
                                              in1=BADA[:, l, :].unsqueeze(2).broadcast_to([128, 144, 2]), op=ALU.add),
              r=(psk, CST), w=(TAB,))
        P.barrier()

        def rsq(ap, trk):
            A(lambda: nc.scalar.activation(out=ap, in_=ap, func=AF.Sqrt), r=(trk,), w=(trk,))
            V(lambda: nc.vector.reciprocal(out=ap, in_=ap), r=(trk,), w=(trk,))

        def stats_rstd(src_fn, trks, rstd_ap, rtrk, sq_pool):
            for c in range(NCH):
                sq, sqk = sq_pool[c % 2]
                A(lambda: nc.scalar.activation(out=sq, in_=src_fn(c), func=AF.Square), r=(trks[c],), w=(sqk,))
                MMs(PSS[:], PSSK, ONESB, sq, c == 0, c == NCH - 1, (sqk, CST), sig=True)
            V(lambda: nc.vector.tensor_scalar(out=rstd_ap, in0=PSS[:], scalar1=1.0 / D, scalar2=EPS, op0=ALU.mult, op1=ALU.add), r=(PSSK,), w=(rtrk,))
            rsq(rstd_ap, rtrk)

        def mod_coefs(l, sub, kind, resw):
            V(lambda: nc.vector.scalar_tensor_tensor(out=ACO[:], in0=MOD[:, l, 3 * sub + 1, :, kind], scalar=1.0, in1=NPRE[:, l * 3 + sub, :], op0=ALU.add, op1=ALU.mult),
              r=(TAB, CST), w=(ACT_,))
            V(lambda: nc.vector.scalar_tensor_tensor(out=COEF[:], in0=MOD[:, l, 3 * sub + 2, :, kind], scalar=float(resw), in1=NPOST[:, l * 3 + sub, :], op0=ALU.mult, op1=ALU.mult),
              r=(TAB, CST), w=(ACT_,))

        def norm_mod(l, sub, kind, half, dst_t, dst_trk, dcol0, bufs=None):
            tk = slice(half * 512, half * 512 + 512)
            if bufs is None:
                TMP.reset()
                RS = TMP.alloc([128, 512], F32); RSK = Trk()
                sqp = [(TMP.alloc([128, 512], BF16), Trk()) for _ in range(2)]
                tp = [(TMP.alloc([128, 512], F32), Trk()) for _ in range(2)]
            else:
                RS, RSK, sqp, tp = bufs
            stats_rstd(lambda c: XTt[:, c, tk], XT, RS[:], RSK, [(s[:], k) for s, k in sqp])
            for c in range(NCH):
                tt, ttk = tp[c % 2]
                V(lambda: nc.vector.tensor_tensor(out=tt[:], in0=XTt[:, c, tk], in1=RS[:], op=ALU.mult), r=(XT[c], RSK), w=(ttk,))
                A(lambda: nc.scalar.activation(out=dst_t[:, c, dcol0:dcol0 + 512], in_=tt[:], func=AF.Identity,
                                               bias=MOD[:, l, 3 * sub, c, kind:kind + 1], scale=ACO[:, c:c + 1]),
                  r=(ttk, ACT_, TAB), w=(dst_trk[c],))

        def post_resid(half, ysrc_fn, ytrks, bufs=None):
            tk = slice(half * 512, half * 512 + 512)
            if bufs is None:
                RS = TMP.alloc([128, 512], F32); RSK = Trk()
                tp = [(TMP.alloc([128, 512], F32), Trk()) for _ in range(2)]
            else:
                RS, RSK, tp = bufs
            V(lambda: nc.vector.tensor_scalar(out=RS[:], in0=PSS[:], scalar1=1.0 / D, scalar2=EPS, op0=ALU.mult, op1=ALU.add), r=(PSSK,), w=(RSK,))
            rsq(RS[:], RSK)
            for c in range(NCH):
                tt, ttk = tp[c % 2]
                V(lambda: nc.vector.tensor_tensor(out=tt[:], in0=ysrc_fn(c), in1=RS[:], op=ALU.mult), r=(ytrks[c], RSK), w=(ttk,))
                V(lambda: nc.vector.scalar_tensor_tensor(out=XTt[:, c, tk], in0=tt[:], scalar=COEF[:, c:c + 1], in1=XTt[:, c, tk], op0=ALU.mult, op1=ALU.add),
                  r=(ttk, ACT_, XT[c]), w=(XT[c],))

        def ffn(l, which, kind):
            sub = 0 if which == 0 else 2
            w1 = ffn_w_in[l, which]
            w2 = ffn_w_out[l, which]
            P.barrier()
            mod_coefs(l, sub, kind, 0.5)
            for half in range(2):
                for j in range(NF):
                    wpush([(w1, 0, 16, j * 128, 128), (w1, 0, 16, FF + j * 128, 128)])
                for c in range(NCH):
                    wpush([(w2, 0, 11, c * 128, 128), (w2, 11 * 128, 11, c * 128, 128)])
                    wpush([(w2, 22 * 128, 11, c * 128, 128), (w2, 33 * 128, 11, c * 128, 128)])
            wkick()
            TMP.reset()
            fRS = TMP.alloc([128, 512], F32); fRSK = Trk()
            fsq = [(TMP.alloc([128, 512], BF16), Trk()) for _ in range(2)]
            ftp = [(TMP.alloc([128, 512], F32), Trk()) for _ in range(2)]
            fRS2 = TMP.alloc([128, 512], F32); fRS2K = Trk()
            ftp2 = [(TMP.alloc([128, 512], F32), Trk()) for _ in range(2)]
            sap = [(TMP.alloc([128, 512], F32), Trk()) for _ in range(2)]
            sqp = [(TMP.alloc([128, 512], BF16), Trk()) for _ in range(2)]
            for half in range(2):
                norm_mod(l, sub, kind, half, HHt, HH, 0, bufs=(fRS, fRSK, fsq, ftp))
                stop_at("nm")
                for j in range(NF):
                    (wa, wb), wt = wtake()
                    pa, pak = nps()
                    pb, pbk = nps()
                    for k in range(16):
                        MMs(pa[:], pak, wa[:, k, :], HHt[:, k, :], k == 0, k == 15, (wt[0], HH[k]))
                    for k in range(16):
                        MMs(pb[:], pbk, wb[:, k, :], HHt[:, k, :], k == 0, k == 15, (wt[1], HH[k]))
                    sa, sak = sap[j % 2]
                    A(lambda: nc.scalar.activation(out=sa[:], in_=pa[:], func=AF.Silu), r=(pak,), w=(sak,))
                    V(lambda: nc.vector.tensor_tensor(out=Gt[:, j, :], in0=sa[:], in1=pb[:], op=ALU.mult), r=(sak, pbk), w=(GG[j],))
                stop_at("g1")
                for c in range(NCH):
                    py, pyk = nps()
                    for hh in range(2):
                        (wv0, wv1), wt = wtake()
                        for k in range(22):
                            j = hh * 22 + k
                            wv = wv0 if k < 11 else wv1
                            MMs(py[:], pyk, wv[:, k % 11, :], Gt[:, j, :], j == 0, j == NF - 1, (wt[0 if k < 11 else 1], GG[j]), sig=(True if k in (10, 21) else None))
                    if os.environ.get("KVAR") == "noevac":
                        continue
                    V(lambda: nc.vector.tensor_copy(out=YFt[:, c, :], in_=py[:]), r=(pyk,), w=(YF[c],))
                    if os.environ.get("KVAR") == "nostat":
                        continue
                    sq, sqk = sqp[c % 2]
                    A(lambda: nc.scalar.activation(out=sq[:], in_=YFt[:, c, :], func=AF.Square), r=(YF[c],), w=(sqk,))
                    if os.environ.get("KVAR") == "nomm":
                        continue
                    MMs(PSS[:], PSSK, ONESB, sq[:], c == 0, c == NCH - 1, (sqk, CST), sig=True)
                stop_at("g2")
                post_resid(half, lambda c: YFt[:, c, :], YF, bufs=(fRS2, fRS2K, ftp2))
                stop_at("h0")

        def proj_fm(view, wt, ncols128, evac):
            for cb in range(ncols128):
                for half in range(2):
                    ps, psk = nps()
                    for k in range(16):
                        MMs(ps[:], psk, view[:, k, cb * 128:(cb + 1) * 128], HTt[:, k, half * 512:(half + 1) * 512], k == 0, k == 15, wt + (HT[k],))
                    evac(cb, half, ps, psk)

        def proj_tm(view, wt, ncols, evac):
            for tt in range(8):
                ps, psk = nps()
                for k in range(16):
                    MMs(ps[:, 0:ncols], psk, HTt[:, k, tt * 128:(tt + 1) * 128], view[:, k, :], k == 0, k == 15, wt + (HT[k],))
                evac(tt, ps, psk)

        def attn_core(Mq, q_ap, qtrk, ksegs, vtiles, out_ap, outk, first_out, W_, SALL, SALLK, PB, PBK, PTT, PTK, sm, smk):
            off = 0
            for (k_ap, n, ktrk, bias, btrk) in ksegs:
                ps, psk = nps()
                MMs(ps[0:Mq, 0:n], psk, q_ap, k_ap, True, True, (qtrk, ktrk))
                if bias is not None:
                    V(lambda: nc.vector.tensor_tensor(out=SALL[0:Mq, off:off + n].rearrange("p (j k) -> p j k", k=64),
                                                      in0=ps[0:Mq, 0:n].rearrange("p (j k) -> p j k", k=64), in1=bias, op=ALU.add),
                      r=(psk, btrk), w=(SALLK,))
                else:
                    A(lambda: nc.scalar.copy(out=SALL[0:Mq, off:off + n], in_=ps[0:Mq, 0:n]), r=(psk,), w=(SALLK,))
                off += n
            assert off == W_
            V(lambda: nc.vector.reduce_max(out=sm[0:Mq, 0:1], in_=SALL[0:Mq, 0:W_], axis=AX.X), r=(SALLK,), w=(smk,))
            V(lambda: nc.vector.tensor_scalar(out=sm[0:Mq, 1:2], in0=sm[0:Mq, 0:1], scalar1=-1.0, scalar2=None, op0=ALU.mult), r=(smk,), w=(smk,))
            A(lambda: nc.scalar.activation(out=SALL[0:Mq, 0:W_], in_=SALL[0:Mq, 0:W_], func=AF.Exp, bias=sm[0:Mq, 1:2], scale=1.0, accum_out=sm[0:Mq, 2:3]),
              r=(SALLK, smk), w=(SALLK, smk))
            V(lambda: nc.vector.reciprocal(out=sm[0:Mq, 3:4], in_=sm[0:Mq, 2:3]), r=(smk,), w=(smk,))
            V(lambda: nc.vector.tensor_scalar(out=PB[0:Mq, 0:W_], in0=SALL[0:Mq, 0:W_], scalar1=sm[0:Mq, 3:4], scalar2=None, op0=ALU.mult), r=(SALLK, smk), w=(PBK,))
            nt = W_ // 128
            ps, psk = nps()
            psb = ps[:, :].bitcast(BF16)
            for t in range(nt):
                TR(psb[:, t * Mq:(t + 1) * Mq], psk, PB[0:Mq, t * 128:(t + 1) * 128], IDB[0:Mq, 0:Mq], (PBK, CST))
            A(lambda: nc.scalar.copy(out=PTT[:, 0:nt * Mq], in_=psb[:, 0:nt * Mq]), r=(psk,), w=(PTK,))
            for t in range(nt):
                v_ap, vtrk = vtiles[t]
                MMs(out_ap, outk, v_ap, PTT[:, t * Mq:(t + 1) * Mq], t == 0, t == nt - 1, (vtrk, PTK))

        def mixer(l, seg):
            kind = seg
            wl = w_in[l]
            P.barrier()
            mod_coefs(l, 1, kind, 1.0)
            TBK = Trk()
            posc = [C_POS + 0, C_POS + 1, C_POS + 2, C_POS + 3]
            for (ti, lo, pc, mul) in ((0, 0, posc[0], SC_RET), (0, 8, posc[1], SC_RET), (1, 0, posc[2], 1.0), (1, 8, posc[3], 1.0)):
                V(lambda: nc.vector.tensor_scalar(out=RTAB[:, ti, lo:lo + 8], in0=NLG[:, l, lo:lo + 8], scalar1=CF[:, pc:pc + 1], scalar2=-1.0, op0=ALU.mult, op1=ALU.mult),
                  r=(TAB, CST), w=(TBK,))
            V(lambda: nc.vector.tensor_scalar(out=RTAB[:, 2, :], in0=NLG[:, l, :], scalar1=-128.0, scalar2=None, op0=ALU.mult), r=(TAB,), w=(TBK,))
            A(lambda: nc.scalar.activation(out=RTAB[:, 0:3, :], in_=RTAB[:, 0:3, :], func=AF.Exp), r=(TBK,), w=(TBK,))
            V(lambda: nc.vector.tensor_scalar(out=RTAB[:, 0, :], in0=RTAB[:, 0, :], scalar1=SC_RET, scalar2=None, op0=ALU.mult), r=(TBK,), w=(TBK,))
            for half in range(2):
                norm_mod(l, 1, kind, half, HTt, HT, half * 512)
            P.barrier()
            stop_at("mnm")
            retention(l, seg, TBK)
            P.barrier()
            stop_at("ret")
            merge(l, w_br[l], 16, GA, True)
            P.barrier()
            stop_at("mg1")
            gmlp(l)
            P.barrier()
            stop_at("gm")
            merge(l, w_bg[l], 8, GB, False)
            P.barrier()
            stop_at("mg2")
            attention(l, seg)
            P.barrier()
            stop_at("att")
            merge(l, w_bn[l], 8, GC, False)
            P.barrier()
            stop_at("mg3")
            for cc in range(8):
                wpush([(w_o[l], 0, 16, cc * 256, 256)])
            TMP.reset()
            sqp = [(TMP.alloc([128, 512], BF16), Trk()) for _ in range(2)]
            PS2, PS2K = PSt[6], PSK[6]
            statb = [(PSS, PSSK), (PS2, PS2K)]
            n = 0
            psn[0] = 6
            psi[0] = 0
            for cc in range(8):
                (wv,), wt = wtake()
                for c2_ in range(2):
                    c = cc * 2 + c2_
                    for half in range(2):
                        py, pyk = nps()
                        for k in range(16):
                            MMs(py[:], pyk, wv[:, k, c2_ * 128:(c2_ + 1) * 128], Mt[:, k, half * 512:(half + 1) * 512], k == 0, k == 15, wt + (MM[k],))
                        V(lambda: nc.vector.tensor_copy(out=Yt[:, c, half * 512:(half + 1) * 512], in_=py[:]), r=(pyk,), w=(YY[c],))
                        sq, sqk = sqp[n % 2]
                        n += 1
                        A(lambda: nc.scalar.activation(out=sq[:], in_=Yt[:, c, half * 512:(half + 1) * 512], func=AF.Square), r=(YY[c],), w=(sqk,))
                        sb, sbk = statb[half]
                        MMs(sb[:], sbk, ONESB, sq[:], c == 0, c == NCH - 1, (sqk, CST), sig=True)
            psi[0] = 0
            psn[0] = 7
            for half in range(2):
                tk = slice(half * 512, half * 512 + 512)
                sb, sbk = statb[half]
                RS = TMP.alloc([128, 512], F32); RSK = Trk()
                tp = [(TMP.alloc([128, 512], F32), Trk()) for _ in range(2)]
                V(lambda: nc.vector.tensor_scalar(out=RS[:], in0=sb[:], scalar1=1.0 / D, scalar2=EPS, op0=ALU.mult, op1=ALU.add), r=(sbk,), w=(RSK,))
                rsq(RS[:], RSK)
                for c in range(NCH):
                    tt, ttk = tp[c % 2]
                    V(lambda: nc.vector.tensor_tensor(out=tt[:], in0=Yt[:, c, tk], in1=RS[:], op=ALU.mult), r=(YY[c], RSK), w=(ttk,))
                    V(lambda: nc.vector.scalar_tensor_tensor(out=XTt[:, c, tk], in0=tt[:], scalar=COEF[:, c:c + 1], in1=XTt[:, c, tk], op0=ALU.mult, op1=ALU.add),
                      r=(ttk, ACT_, XT[c]), w=(XT[c],))

        def merge(l, Wb, K, gcol, first):
            wl = w_in[l]
            for c in range(NCH):
                wpush([(wl, 0, 16, gcol + c * 128, 128), (Wb, 0, K, c * 128, 128)])
            TMP.reset()
            sgp = [(TMP.alloc([128, 512], F32), Trk()) for _ in range(2)]
            tp = [(TMP.alloc([128, 512], F32), Trk()) for _ in range(2)]
            n = 0
            for c in range(NCH):
                (wg, wb), wt = wtake()
                for half in range(2):
                    tk = slice(half * 512, half * 512 + 512)
                    pg, pgk = nps()
                    pp, ppk = nps()
                    for k in range(16):
                        MMs(pg[:], pgk, wg[:, k, :], HTt[:, k, tk], k == 0, k == 15, (wt[0], HT[k]))
                    for k in range(K):
                        MMs(pp[:], ppk, wb[:, k, :], BRt[:, k, tk], k == 0, k == K - 1, (wt[1], BR[k]))
                    sg, sgk = sgp[n % 2]
                    A(lambda: nc.scalar.activation(out=sg[:], in_=pg[:], func=AF.Sigmoid), r=(pgk,), w=(sgk,))
                    if first:
                        V(lambda: nc.vector.tensor_tensor(out=Mt[:, c, tk], in0=sg[:], in1=pp[:], op=ALU.mult), r=(sgk, ppk), w=(MM[c],))
                    else:
                        tt, ttk = tp[n % 2]
                        V(lambda: nc.vector.tensor_tensor(out=tt[:], in0=sg[:], in1=pp[:], op=ALU.mult), r=(sgk, ppk), w=(ttk,))
                        V(lambda: nc.vector.tensor_tensor(out=Mt[:, c, tk], in0=tt[:], in1=Mt[:, c, tk], op=ALU.add), r=(ttk, MM[c]), w=(MM[c],))
                    n += 1

        def retention(l, seg, TBK):
            wl = w_in[l]
            nseq, nchs = (4, 2) if seg == 0 else (1, 8)
            for hd in range(8):
                wpush([(wl, 0, 16, RQ + hd * 128, 128), (wl, 0, 16, RK + hd * 128, 128)])
                wpush([(wl, 0, 16, RV + hd * 256, 256)])
                wpush([(wl, 0, 16, RG + hd * 256, 256)])
            RTR.reset()
            TMP.reset()
            QT = RTR.alloc([128, T], BF16); QTK = Trk()
            KT = RTR.alloc([128, T], BF16); KTK = Trk()
            VT = RTR.alloc([128, 8, 256], BF16); VTK = Trk()
            SGT = RTR.alloc([128, 8, 256], BF16); SGK = Trk()
            KZF = RTR.alloc([128, 8, 128], BF16); KZFK = Trk()
            KZB = RTR.alloc([128, 8, 128], BF16); KZBK = Trk()
            SFB = RTR.alloc([128, 8, 256], BF16); SFK = Trk()
            SBB = RTR.alloc([128, 8, 256], BF16); SBK = Trk()
            SF = RTR.alloc([128, 256], F32); SFFK = dtrk('sf')
            SB = RTR.alloc([128, 256], F32); SBFK = dtrk('sb')
            MT = RTR.alloc([128, 128], F32); MTK = Trk()
            EA = RTR.alloc([128, 128], F32); EAK = Trk()
            PTb = [(RTR.alloc([128, 128], BF16), Trk()) for _ in range(2)]
            O1 = [(TMP.alloc([128, 256], F32), Trk()) for _ in range(2)]
            RTk = [(TMP.alloc([128, 256], BF16), Trk()) for _ in range(2)]
            JNK = TMP.alloc([128, 256], F32); JNKK = Trk()
            sm = TMP.alloc([128, 8, 4], F32); smk = Trk()
            XF = [(TMP.alloc([128, 512], F32), Trk()) for _ in range(2)]
            T1 = [(TMP.alloc([128, 512], F32), Trk()) for _ in range(2)]
            T2 = [(TMP.alloc([128, 512], F32), Trk()) for _ in range(2)]
            for hd in range(8):
                V(lambda: nc.vector.tensor_scalar(out=SMALL[:, 0:1], in0=NLG[:, l, hd:hd + 1], scalar1=-1.0, scalar2=None, op0=ALU.mult), r=(TAB,), w=(smk,))
                V(lambda: nc.vector.tensor_scalar(out=SMALL[:, 1:2], in0=NLG[:, l, 8 + hd:9 + hd], scalar1=-1.0, scalar2=None, op0=ALU.mult), r=(TAB,), w=(smk,))
                A(lambda: nc.scalar.activation(out=EA[:], in_=CF[:, C_DP:C_DP + 128], func=AF.Exp, scale=SMALL[:, 0:1]), r=(CST, smk), w=(EAK,))
                V(lambda: nc.vector.tensor_tensor(out=MT[:], in0=EA[:], in1=CF[:, C_MGE:C_MGE + 128], op=ALU.mult), r=(EAK, CST), w=(MTK,))
                A(lambda: nc.scalar.activation(out=EA[:], in_=CF[:, C_DN:C_DN + 128], func=AF.Exp, scale=SMALL[:, 1:2]), r=(CST, smk, MTK), w=(EAK,))
                V(lambda: nc.vector.tensor_tensor(out=EA[:], in0=EA[:], in1=CF[:, C_MLE:C_MLE + 128], op=ALU.mult), r=(EAK, CST), w=(EAK,))
                V(lambda: nc.vector.tensor_tensor(out=MT[:], in0=MT[:], in1=EA[:], op=ALU.add), r=(EAK, MTK), w=(MTK,))
                (wq_, wk_), wt = wtake()
                n = 0
                for (wv, dstT, dstK) in ((wq_, QT, QTK), (wk_, KT, KTK)):
                    for half in range(2):
                        tk = slice(half * 512, half * 512 + 512)
                        ps, psk = nps()
                        for k in range(16):
                            MMs(ps[:], psk, wv[:, k, :], HTt[:, k, tk], k == 0, k == 15, wt + (HT[k],))
                        if seg == 0:
                            A(lambda: nc.scalar.copy(out=dstT[:, tk], in_=ps[:]), r=(psk,), w=(dstK,))
                        else:
                            xf, xfk = XF[n % 2]
                            t1, t1k = T1[n % 2]
                            t2, t2k = T2[n % 2]
                            n += 1
                            A(lambda: nc.scalar.copy(out=xf[:], in_=ps[:]), r=(psk,), w=(xfk,))
                            p2, p2k = nps()
                            MMs(p2[:], p2k, PERM, xf[:], True, True, (xfk, CST))
                            r0 = half * 8
                            for (pl, ph, cos_ap, sin_ap) in (
                                (0, 64, CF[0:64, C_CR + r0:C_CR + r0 + 8].unsqueeze(2).broadcast_to([64, 8, 64]),
                                 CF[0:64, C_SR + r0:C_SR + r0 + 8].unsqueeze(2).broadcast_to([64, 8, 64])),
                                (64, 128, CF[64:128, C_CR:C_CR + 64].unsqueeze(1).broadcast_to([64, 8, 64]),
                                 CF[64:128, C_SR:C_SR + 64].unsqueeze(1).broadcast_to([64, 8, 64]))):
                                v3 = lambda ap: ap.rearrange("p (r c) -> p r c", c=64)
                                V(lambda: nc.vector.tensor_tensor(out=v3(t1[pl:ph, :]), in0=v3(xf[pl:ph, :]), in1=cos_ap, op=ALU.mult), r=(xfk, CST), w=(t1k,))
                                V(lambda: nc.vector.tensor_tensor(out=v3(t2[pl:ph, :]), in0=v3(p2[pl:ph, :]), in1=sin_ap, op=ALU.mult), r=(p2k, CST), w=(t2k,))
                            V(lambda: nc.vector.tensor_tensor(out=dstT[:, tk], in0=t1[:], in1=t2[:], op=ALU.add), r=(t1k, t2k), w=(dstK,))
                (wv,), wt = wtake()
                proj_tm(wv, wt, 256, lambda tt, ps, psk: A(lambda: nc.scalar.copy(out=VT[:, tt, :], in_=ps[:, 0:256]), r=(psk,), w=(VTK,)))
                (wv,), wt = wtake()
                proj_tm(wv, wt, 256, lambda tt, ps, psk: A(lambda: nc.scalar.activation(out=SGT[:, tt, :], in_=ps[:, 0:256], func=AF.Silu), r=(psk,), w=(SGK,)))
                for tt in range(8):
                    ps, psk = nps()
                    psb = ps[:, :].bitcast(BF16)
                    TR(psb[:, 0:128], psk, KT[:, tt * 128:(tt + 1) * 128], IDB, (KTK, CST))
                    V(lambda: nc.vector.tensor_scalar(out=KZF[:, tt, :], in0=psb[:, 0:128], scalar1=RTAB[:, 0, hd:hd + 1], scalar2=None, op0=ALU.mult), r=(psk, TBK), w=(KZFK,))
                    V(lambda: nc.vector.tensor_scalar(out=KZB[:, tt, :], in0=psb[:, 0:128], scalar1=RTAB[:, 0, 8 + hd:9 + hd], scalar2=None, op0=ALU.mult), r=(psk, TBK), w=(KZBK,))
                for s in range(nseq):
                    chunks = list(range(s * nchs, (s + 1) * nchs))
                    for (dr, S_, S_K, KZ, KZK, SXB, SXK, order) in ((0, SF, SFFK, KZF, KZFK, SFB, SFK, chunks), (1, SB, SBFK, KZB, KZBK, SBB, SBK, chunks[::-1])):
                        if seg == 1:
                            P.dma("sp", S_[:], sr[l, dr, hd], S_K, w=(S_K,))
                        else:
                            V(lambda: nc.vector.memset(S_[:], 0.0), w=(S_K,))
                        for n_ in order:
                            A(lambda: nc.scalar.copy(out=SXB[:, n_, :], in_=S_[:]), r=(S_K,), w=(SXK,))
                            ps, psk = nps()
                            MMs(ps[:, 0:256], psk, KZ[:, n_, :], VT[:, n_, :], True, True, (KZK, VTK))
                            V(lambda: nc.vector.scalar_tensor_tensor(out=S_[:], in0=S_[:], scalar=RTAB[:, 2, dr * 8 + hd:dr * 8 + hd + 1], in1=ps[:, 0:256], op0=ALU.mult, op1=ALU.add),
                              r=(S_K, psk, TBK), w=(S_K,))
                        if seg == 0:
                            P.dma("sp", ns[s, l, dr, hd], S_[:], S_K, r=(S_K,))
                for n_ in range(nseq * nchs):
                    ck_ = slice(n_ * 128, (n_ + 1) * 128)
                    ps, psk = nps()
                    MMs(ps[:, 0:128], psk, KT[:, ck_], QT[:, ck_], True, True, (KTK, QTK))
                    pt, ptk = PTb[n_ % 2]
                    V(lambda: nc.vector.tensor_tensor(out=pt[:], in0=ps[:, 0:128], in1=MT[:], op=ALU.mult), r=(psk, MTK), w=(ptk,))
                    po, pok = nps()
                    pc, pck = nps()
                    MMs(po[:, 0:256], pok, pt[:], VT[:, n_, :], True, True, (ptk, VTK))
                    MMs(po[:, 256:512], pok, QT[:, ck_], SFB[:, n_, :], True, True, (QTK, SFK))
                    MMs(pc[:, 0:256], pck, QT[:, ck_], SBB[:, n_, :], True, True, (QTK, SBK))
                    o1, o1k = O1[n_ % 2]
                    V(lambda: nc.vector.tensor_scalar(out=o1[:], in0=po[:, 256:512], scalar1=RTAB[:, 1, hd:hd + 1], scalar2=None, op0=ALU.mult), r=(pok, TBK), w=(o1k,))
                    V(lambda: nc.vector.scalar_tensor_tensor(out=o1[:], in0=pc[:, 0:256], scalar=RTAB[:, 1, 8 + hd:9 + hd], in1=o1[:], op0=ALU.mult, op1=ALU.add), r=(pck, TBK, o1k), w=(o1k,))
                    V(lambda: nc.vector.tensor_tensor(out=o1[:], in0=po[:, 0:256], in1=o1[:], op=ALU.add), r=(pok, o1k), w=(o1k,))
                    j4 = n_ % 8
                    A(lambda: nc.scalar.activation(out=JNK[:], in_=o1[:], func=AF.Square, accum_out=sm[:, j4, 0:1]), r=(o1k,), w=(JNKK, smk))
                    V(lambda: nc.vector.tensor_scalar(out=sm[:, j4, 1:2], in0=sm[:, j4, 0:1], scalar1=1.0 / 256, scalar2=EPS, op0=ALU.mult, op1=ALU.add), r=(smk,), w=(smk,))
                    A(lambda: nc.scalar.activation(out=sm[:, j4, 2:3], in_=sm[:, j4, 1:2], func=AF.Sqrt), r=(smk,), w=(smk,))
                    V(lambda: nc.vector.reciprocal(out=sm[:, j4, 2:3], in_=sm[:, j4, 2:3]), r=(smk,), w=(smk,))
                    rt, rtk = RTk[n_ % 2]
                    V(lambda: nc.vector.scalar_tensor_tensor(out=rt[:], in0=o1[:], scalar=sm[:, j4, 2:3], in1=SGT[:, n_, :], op0=ALU.mult, op1=ALU.mult), r=(o1k, smk, SGK), w=(rtk,))
                    p3, p3k = nps()
                    p3b = p3[:, :].bitcast(BF16)
                    for e2 in range(2):
                        TR(p3b[:, e2 * 128:(e2 + 1) * 128], p3k, rt[:, e2 * 128:(e2 + 1) * 128], IDB, (rtk, CST))
                    A(lambda: nc.scalar.copy(out=BRt[:, hd * 2:hd * 2 + 2, ck_], in_=p3b[:, 0:256].rearrange("p (e t) -> p e t", t=128)), r=(p3k,), w=(BR[hd * 2], BR[hd * 2 + 1]))

        def gmlp(l):
            wl = w_in[l]
            for s in range(4):
                wpush([(wl, 0, 16, GV + s * 256, 256)])
            for g in range(4):
                wpush([(wl, 0, 16, GU + g * 256, 256)])
            BRU.reset(pool=True)
            TMP.reset()
            VTK_ = BRU.alloc([128, 8, 1024], BF16); VK = [Trk() for _ in range(8)]
            GNB = TMP.alloc([128, 1024], F32); GK = dtrk('gk')
            WSN = TMP.alloc([128, 4, 128], BF16)
            WST = TMP.alloc([128, 4, 128], BF16); WK = dtrk('wk')
            BS = TMP.alloc([128, 4], F32)
            ST = TMP.alloc([128, 8, 4, 6], F32); STK = Trk()
            MV = TMP.alloc([128, 8, 4], F32); MVK = Trk()
            UB = [(TMP.alloc([128, 256], BF16), Trk()) for _ in range(2)]
            GTb = [(TMP.alloc([128, 256], BF16), Trk()) for _ in range(2)]
            P.dma("sp", GNB[:], gm_norm[l:l + 1, :].broadcast_to([128, 1024]), GK, w=(GK,), nonc=True)
            P.dma("pool", WSN[:], gm_ws[l].rearrange("g i j -> i g j"), WK, w=(WK,))
            P.dma("sp", BS[:], gm_bs[l].rearrange("g i -> i g"), GK, w=(GK,), nonc=True)
            ps, psk = nps()
            psb = ps[:, :].bitcast(BF16)
            for g in range(4):
                TR(psb[:, g * 128:(g + 1) * 128], psk, WSN[:, g, :], IDB, (WK, CST))
            WTK = Trk()
            A(lambda: nc.scalar.copy(out=WST[:].rearrange("p g i -> p (g i)"), in_=psb[:, 0:512]), r=(psk,), w=(WTK,))
            for s in range(4):
                (wv,), wt = wtake()

                def ev(tt, ps, psk, s=s):
                    V(lambda: nc.vector.tensor_copy(out=VTK_[:, tt, s * 256:(s + 1) * 256], in_=ps[:, 0:256]), r=(psk,), w=(VK[tt],))
                    V(lambda: nc.vector.bn_stats(out=ST[:, tt, s, :], in_=ps[:, 0:256]), r=(psk,), w=(STK,))
                proj_tm(wv, wt, 256, ev)
            for tt in range(8):
                V(lambda: nc.vector.bn_aggr(out=MV[:, tt, 0:2], in_=ST[:, tt].rearrange("p s x -> p (s x)")), r=(STK,), w=(MVK,))
                V(lambda: nc.vector.tensor_scalar(out=MV[:, tt, 2:3], in0=MV[:, tt, 1:2], scalar1=EPS, scalar2=None, op0=ALU.add), r=(MVK,), w=(MVK,))
                rsq(MV[:, tt, 2:3], MVK)
                V(lambda: nc.vector.tensor_scalar(out=VTK_[:, tt, :], in0=VTK_[:, tt, :], scalar1=MV[:, tt, 0:1], scalar2=MV[:, tt, 2:3], op0=ALU.subtract, op1=ALU.mult), r=(MVK, VK[tt]), w=(VK[tt],))
                V(lambda: nc.vector.tensor_tensor(out=VTK_[:, tt, :], in0=VTK_[:, tt, :], in1=GNB[:], op=ALU.mult), r=(VK[tt], GK), w=(VK[tt],))
            n = 0
            for g in range(4):
                (wv,), wt = wtake()

                def eu(tt, ps, psk, g=g):
                    i = n_[0] % 2
                    n_[0] += 1
                    ub, ubk = UB[i]
                    gt, gtk = GTb[i]
                    A(lambda: nc.scalar.copy(out=ub[:], in_=ps[:, 0:256]), r=(psk,), w=(ubk,))
                    pm, pmk = nps()
                    MMs(pm[:, 0:256], pmk, WST[:, g, :], VTK_[:, tt, g * 256:(g + 1) * 256], True, True, (WTK, VK[tt]))
                    V(lambda: nc.vector.scalar_tensor_tensor(out=gt[:], in0=pm[:, 0:256], scalar=BS[:, g:g + 1], in1=ub[:], op0=ALU.add, op1=ALU.mult), r=(pmk, GK, ubk), w=(gtk,))
                    p3, p3k = nps()
                    p3b = p3[:, :].bitcast(BF16)
                    for e2 in range(2):
                        TR(p3b[:, e2 * 128:(e2 + 1) * 128], p3k, gt[:, e2 * 128:(e2 + 1) * 128], IDB, (gtk, CST))
                    A(lambda: nc.scalar.copy(out=BRt[:, g * 2:g * 2 + 2, tt * 128:(tt + 1) * 128], in_=p3b[:, 0:256].rearrange("p (e t) -> p e t", t=128)), r=(p3k,), w=(BR[g * 2], BR[g * 2 + 1]))
                n_ = [0]
                proj_tm(wv, wt, 256, eu)

        def attention(l, seg):
            wl = w_in[l]
            BRU.reset(pool=True)
            TMP.reset()
            psn[0] = 5
            psi[0] = 0
            sc = 128.0 ** -0.5
            if seg == 0:
                for s in range(4):
                    wpush([(wl, 0, 16, NK + s * 256, 256)])
                for s in range(4):
                    wpush([(wl, 0, 16, NV + s * 256, 256)])
                for hd in range(8):
                    wpush([(wl, 0, 16, NQ + hd * 128, 128), (wl, 0, 16, NK + hd * 128, 128)])
                VTOK = BRU.alloc([128, 8, 1024], BF16); VTK = Trk()
                STG = [(TMP.alloc([128, 256], F32), dtrk('kv%d' % _)) for _ in range(2)]
                QT = TMP.alloc([128, T], BF16); QTK = Trk()
                KT = TMP.alloc([128, T], BF16); KTK = Trk()
                SALL = TMP.alloc([128, 256], F32); SALLK = Trk()
                PB = TMP.alloc([128, 256], BF16); PBK = Trk()
                PTT = TMP.alloc([128, 256], BF16); PTK = Trk()
                sm = TMP.alloc([128, 4], F32); smk = Trk()
                cnt = [0]
                for (isv, dst) in ((0, nk), (1, nv)):
                    for s in range(4):
                        (wv,), wt = wtake()

                        def ev(tt, ps, psk, s=s, isv=isv, dst=dst):
                            sg, sgk = STG[cnt[0] % 2]
                            cnt[0] += 1
                            V(lambda: nc.vector.tensor_copy(out=sg[:], in_=ps[:, 0:256]), r=(psk,), w=(sgk,))
                            if isv:
                                A(lambda: nc.scalar.copy(out=VTOK[:, tt, s * 256:(s + 1) * 256], in_=sg[:]), r=(sgk,), w=(VTK,))
                            b, t2 = tt // 2, tt % 2
                            P.dma("sp", dst[b, l, t2 * 128:(t2 + 1) * 128, s * 256:(s + 1) * 256], sg[:], sgk, r=(sgk,))
                        proj_tm(wv, wt, 256, ev)
                for hd in range(8):
                    (wq_, wk_), wt = wtake()
                    proj_fm(wq_, wt, 1, lambda cb, half, ps, psk: A(lambda: nc.scalar.activation(out=QT[:, half * 512:(half + 1) * 512], in_=ps[:], func=AF.Copy, scale=sc), r=(psk,), w=(QTK,)))
                    proj_fm(wk_, wt, 1, lambda cb, half, ps, psk: A(lambda: nc.scalar.copy(out=KT[:, half * 512:(half + 1) * 512], in_=ps[:]), r=(psk,), w=(KTK,)))
                    for half in range(2):
                        po, pok = npo()
                        for i4 in range(4):
                            s = half * 2 + i4 // 2
                            qt = i4 % 2
                            q0 = s * 256 + qt * 128
                            attn_core(128, QT[:, q0:q0 + 128], QTK, [(KT[:, s * 256:(s + 1) * 256], 256, KTK, None, None)],
                                      [(VTOK[:, s * 2 + t, hd * 128:(hd + 1) * 128], VTK) for t in range(2)],
                                      po[:, i4 * 128:(i4 + 1) * 128], pok, None, 256, SALL, SALLK, PB, PBK, PTT, PTK, sm, smk)
                        A(lambda: nc.scalar.copy(out=BRt[:, hd, half * 512:(half + 1) * 512], in_=po[:]), r=(pok,), w=(BR[hd],))
                psn[0] = 7
                psi[0] = 0
            else:
                for hd in range(8):
                    wpush([(wl, 0, 16, NQ + hd * 128, 128), (wl, 0, 16, NK + hd * 128, 128)])
                    wpush([(wl, 0, 16, NV + hd * 128, 128)])
                EH = BRU.alloc([128, 4096], BF16); EHK = dtrk('eh')
                CKt = BRU.alloc([128, 2, 1024], BF16); CKK = dtrk('ck')
                CVt = BRU.alloc([128, 2, 1024], BF16); CVK = dtrk('cv')
                P.dma("sp", EH[0:32, :], ehot, EHK, w=(EHK,))
                P.dma("pool", CKt[:], ck[l].rearrange("(t p) f -> p t f", p=128), CKK, w=(CKK,))
                P.dma("pool", CVt[:], cv[l].rearrange("(t p) f -> p t f", p=128), CVK, w=(CVK,))
                RBF = TMP.alloc([128, 8, 15], F32); RBK = dtrk('rb')
                RBB = TMP.alloc([128, 8, 16], BF16); RBBK = Trk()
                P.dma("sp", RBF[0:31], na_rpb[l].rearrange("h r c -> c h r"), RBK, w=(RBK,), nonc=True)
                V(lambda: nc.vector.memset(RBB[0:32], 0.0), w=(RBBK,))
                V(lambda: nc.vector.tensor_copy(out=RBB[0:31, :, 0:15], in_=RBF[0:31]), r=(RBK,), w=(RBBK,))
                QT = TMP.alloc([128, T], BF16); QTK = Trk()
                KT = TMP.alloc([128, T], BF16); KTK = Trk()
                VTt = TMP.alloc([128, T], BF16); VTTK = Trk()
                VA = TMP.alloc([128, 8, 128], BF16); VAK = Trk()
                VB = TMP.alloc([128, 7, 128], BF16); VBK = Trk()
                KCT = TMP.alloc([128, 256], BF16); KCK = Trk()
                BH = TMP.alloc([64, 64, 16], F32); BHK = Trk()
                SALL = TMP.alloc([64, 768], F32); SALLK = Trk()
                PB = TMP.alloc([64, 768], BF16); PBK = Trk()
                PTT = TMP.alloc([128, 384], BF16); PTK = Trk()
                sm = TMP.alloc([64, 4], F32); smk = Trk()
                for hd in range(8):
                    (wq_, wk_), wt = wtake()
                    proj_fm(wq_, wt, 1, lambda cb, half, ps, psk: A(lambda: nc.scalar.activation(out=QT[:, half * 512:(half + 1) * 512], in_=ps[:], func=AF.Copy, scale=sc), r=(psk,), w=(QTK,)))
                    proj_fm(wk_, wt, 1, lambda cb, half, ps, psk: A(lambda: nc.scalar.copy(out=KT[:, half * 512:(half + 1) * 512], in_=ps[:]), r=(psk,), w=(KTK,)))
                    (wv_,), wt = wtake()
                    proj_fm(wv_, wt, 1, lambda cb, half, ps, psk: A(lambda: nc.scalar.copy(out=VTt[:, half * 512:(half + 1) * 512], in_=ps[:]), r=(psk,), w=(VTTK,)))
                    for (dstV, dstK, ntile, toff) in ((VA, VAK, 8, 0), (VB, VBK, 7, 64)):
                        for t0 in range(0, ntile, 4):
                            ps, psk = nps()
                            psb = ps[:, :].bitcast(BF16)
                            nn = min(4, ntile - t0)
                            for t in range(nn):
                                TR(psb[:, t * 128:(t + 1) * 128], psk, VTt[:, toff + (t0 + t) * 128: toff + (t0 + t + 1) * 128], IDB, (VTTK, CST))
                            A(lambda: nc.scalar.copy(out=dstV[:, t0:t0 + nn, :].rearrange("p t d -> p (t d)"), in_=psb[:, 0:nn * 128]), r=(psk,), w=(dstK,))
                    ps, psk = nps()
                    psb = ps[:, :].bitcast(BF16)
                    for t in range(2):
                        TR(psb[:, t * 128:(t + 1) * 128], psk, CKt[:, t, hd * 128:(hd + 1) * 128], IDB, (CKK, CST))
                    A(lambda: nc.scalar.copy(out=KCT[:], in_=psb[:, 0:256]), r=(psk,), w=(KCK,))
                    pB0, pB0k = nps()
                    pB1, pB1k = nps()
                    for kc in range(64):
                        pB, pBk = (pB0, pB0k) if kc < 32 else (pB1, pB1k)
                        o = (kc % 32) * 16
                        MMs(pB[0:64, o:o + 16], pBk, EH[0:32, kc * 64:(kc + 1) * 64], RBB[0:32, hd, 0:16], True, True, (EHK, RBBK))
                    for hb, (pB, pBk) in enumerate(((pB0, pB0k), (pB1, pB1k))):
                        V(lambda: nc.vector.tensor_tensor(out=BH[:, hb * 32:(hb + 1) * 32, 0:15], in0=pB[0:64, :].rearrange("p (k r) -> p k r", r=16)[:, :, 0:15],
                                                          in1=CF[0:64, C_NEGM + hb * 32:C_NEGM + (hb + 1) * 32].unsqueeze(2).broadcast_to([64, 32, 15]), op=ALU.add),
                          r=(pBk, CST), w=(BHK,))
                    for rh in range(2):
                        po, pok = npo()
                        for r8 in range(8):
                            r_ = rh * 8 + r8
                            st = min(max(r_ - 4, 0), 8)
                            ro0 = st - r_ + 7
                            bias = BH[:, :, ro0:ro0 + 8].rearrange("p k j -> p j k")
                            if st % 2 == 0:
                                vt = [(VA[:, st // 2 + t, :], VAK) for t in range(4)]
                            else:
                                vt = [(VB[:, (st - 1) // 2 + t, :], VBK) for t in range(4)]
                            vt += [(CVt[:, t, hd * 128:(hd + 1) * 128], CVK) for t in range(2)]
                            attn_core(64, QT[:, r_ * 64:(r_ + 1) * 64], QTK,
                                      [(KT[:, st * 64:st * 64 + 512], 512, KTK, bias, BHK), (KCT[:], 256, KCK, None, None)],
                                      vt, po[:, r8 * 64:(r8 + 1) * 64], pok, None, 768, SALL, SALLK, PB, PBK, PTT, PTK, sm, smk)
                        A(lambda: nc.scalar.copy(out=BRt[:, hd, rh * 512:(rh + 1) * 512], in_=po[:]), r=(pok,), w=(BR[hd],))
            psn[0] = 7
            psi[0] = 0

        def load_x(seg):
            P.barrier()
            TMP.reset()
            STG = [(TMP.alloc([128, D], F32), dtrk('xs%d' % _)) for _ in range(2)]
            for tt in range(8):
                sg, sgk = STG[tt % 2]
                P.dma("sp", sg[:], xin[seg][tt * 128:(tt + 1) * 128, :], sgk, w=(sgk,))
                for c0 in range(0, NCH, 4):
                    ps, psk = nps()
                    for c in range(4):
                        TR(ps[:, c * 128:(c + 1) * 128], psk, sg[:, (c0 + c) * 128:(c0 + c + 1) * 128], IDF, (sgk, CST))
                    V(lambda: nc.vector.tensor_copy(out=XTt[:, c0:c0 + 4, tt * 128:(tt + 1) * 128], in_=ps[:].rearrange("p (c t) -> p c t", t=128)),
                      r=(psk,), w=tuple(XT[c0:c0 + 4]))

        def store_x(seg):
            P.barrier()
            TMP.reset()
            STG = [(TMP.alloc([128, D], F32), dtrk('xs%d' % _)) for _ in range(2)]
            for tt in range(8):
                sg, sgk = STG[tt % 2]
                for c0 in range(0, NCH, 4):
                    ps, psk = nps()
                    for c in range(4):
                        TR(ps[:, c * 128:(c + 1) * 128], psk, XTt[:, c0 + c, tt * 128:(tt + 1) * 128], IDF, (XT[c0 + c], CST))
                    V(lambda: nc.vector.tensor_copy(out=sg[:, c0 * 128:(c0 + 4) * 128], in_=ps[:]), r=(psk,), w=(sgk,))
                P.dma("sp", yout[seg][tt * 128:(tt + 1) * 128, :], sg[:], sgk, r=(sgk,))

        stages = os.environ.get("KSTAGES", "all")
        stopped = False
        for seg in [int(c_) for c_ in os.environ.get("KSEGS", "01")]:
            load_x(seg)
            try:
                for l in range(2):
                    if stages in ("all", "ffn"):
                        ffn(l, 0, seg)
                        stop_at("f0")
                    if stages in ("all", "mix"):
                        mixer(l, seg)
                        stop_at("m0")
                    if stages in ("all", "ffn"):
                        ffn(l, 1, seg)
                    stop_at("l0")
                stop_at("s0")
            except Stop:
                stopped = True
            store_x(seg)
            if stopped:
                break
        if not stopped:
            assert wstate["taken"] == len(wq), (wstate, len(wq))
        P.finish()
        print("instructions:", P.nins, "sems:", len(P.sems), "dma_tot:", P.dma_tot, "eng:", {k: v["count"] for k, v in P.eng.items()})
    return nc


def _consts():
    cf = np.zeros((128, NCF), np.float32)
    cf[:, C_IDF:C_IDF + 128] = np.eye(128, dtype=np.float32)
    pm = np.zeros((128, 128), np.float32)
    for m in range(128):
        blk = m // 32
        partner = m + 32 if blk % 2 == 0 else m - 32
        pm[partner, m] = 1.0
    cf[:, C_PERM:C_PERM + 128] = pm
    inv = (10000.0 ** (-np.arange(0, 64, 2, dtype=np.float32) / 64.0)).astype(np.float32)
    for d in range(128):
        f = d % 32
        sgn = -1.0 if (d % 64) < 32 else 1.0
        npos = 16 if d < 64 else 64
        pos = np.arange(npos, dtype=np.float32)
        ang = (pos * inv[f]).astype(np.float32)
        cf[d, C_CR:C_CR + npos] = np.cos(ang)
        cf[d, C_SR:C_SR + npos] = sgn * np.sin(ang)
    j = np.arange(128)[:, None].astype(np.float32)
    i = np.arange(128)[None, :].astype(np.float32)
    dd = i - j
    cf[:, C_DP:C_DP + 128] = np.maximum(dd, 0)
    cf[:, C_DN:C_DN + 128] = np.maximum(-dd, 0)
    cf[:, C_MGE:C_MGE + 128] = SC_RET * (dd >= 0)
    cf[:, C_MLE:C_MLE + 128] = SC_RET * (dd <= 0)
    q = np.arange(64)
    cs = np.clip(q - 8, 0, 48)
    kc = np.arange(64)
    valid = (kc[None, :] >= cs[:, None]) & (kc[None, :] < cs[:, None] + 16)
    negm = np.where(valid, 0.0, NEGB).astype(np.float32)
    cf[0:64, C_NEGM:C_NEGM + 64] = negm
    cf[64:128, C_NEGM:C_NEGM + 64] = negm
    p = np.arange(128, dtype=np.float32)
    cf[:, C_POS + 0] = 127 - p
    cf[:, C_POS + 1] = p
    cf[:, C_POS + 2] = p + 1
    cf[:, C_POS + 3] = 128 - p
    cb = np.zeros((128, 256), np.float32)
    cb[:, 0:128] = np.eye(128)
    cb[:, 128:256] = 1.0
    cb = cb.astype(ml_dtypes.bfloat16)
    eh = np.zeros((31, 64, 64), np.float32)
    for kcc in range(64):
        for qq in range(64):
            ci = kcc - qq + 15
            if 0 <= ci <= 30:
                eh[ci, kcc, qq] = 1.0
    eh = np.concatenate([eh.reshape(31, 4096), np.zeros((1, 4096), np.float32)], axis=0).astype(ml_dtypes.bfloat16)
    return cf, cb, eh


_CACHE = {}


def kernel(x_prompt, x_sample, cache_k, cache_v, state_ret, c, c_ctx, w_ada, b_ada, norm_pre, norm_post,
           ffn_w_in, ffn_w_out, w_in, ret_decay_logit, gm_norm, gm_ws, gm_bs, na_rpb,
           w_branch_ret, w_branch_gm, w_branch_na, w_out):
    f = lambda a: np.ascontiguousarray(np.asarray(a), dtype=np.float32)
    dbg = os.environ.get("KDBG")
    if "nc" not in _CACHE:
        _CACHE["nc"] = build_program(dbg)
    nc = _CACHE["nc"]
    cf, cb, eh = _consts()
    shared = dict(w_ada=f(w_ada), b_ada=f(b_ada), norm_pre=f(norm_pre), norm_post=f(norm_post), ffn_w_in=f(ffn_w_in),
                  ffn_w_out=f(ffn_w_out), w_in=f(w_in), ret_decay_logit=f(ret_decay_logit).reshape(2, 16), gm_norm=f(gm_norm),
                  gm_ws=f(gm_ws), gm_bs=f(gm_bs), na_rpb=f(na_rpb), w_branch_ret=f(w_branch_ret), w_branch_gm=f(w_branch_gm),
                  w_branch_na=f(w_branch_na), w_out=f(w_out), cstf=cf, cstb=cb, ehot=eh)
    xp, xs, ckk, cvv, srr, cc, cx = f(x_prompt), f(x_sample), f(cache_k), f(cache_v), f(state_ret), f(c), f(c_ctx)
    in_maps = []
    for i in range(8):
        m = dict(shared)
        m["xp"] = xp[4 * i:4 * i + 4].reshape(T, D)
        m["xs"] = xs[i]
        m["ck"] = ckk[i].reshape(2, 256, 1024)
        m["cv"] = cvv[i].reshape(2, 256, 1024)
        m["sr"] = srr[i]
        m["c2"] = np.stack([cx, cc[i]], axis=0)
        in_maps.append(m)
    res = run_bass_kernel_spmd(nc, in_maps, core_ids=list(range(8)))
    R = res.results
    y_prompt = np.concatenate([R[i]["yp"].reshape(4, 256, D) for i in range(8)], axis=0)
    y_sample = np.stack([R[i]["ys"] for i in range(8)], axis=0)
    nkk = np.concatenate([R[i]["nk"].reshape(4, 2, 256, 8, 128) for i in range(8)], axis=0)
    nvv = np.concatenate([R[i]["nv"].reshape(4, 2, 256, 8, 128) for i in range(8)], axis=0)
    nss = np.concatenate([R[i]["ns"] for i in range(8)], axis=0)
    return (y_prompt.astype(np.float32), y_sample.astype(np.float32), nkk.astype(np.float32), nvv.astype(np.float32), nss.astype(np.float32))
```
